# Optimizing a Trainium2 kernel written in Bass

```python
import math
import jax, jax.numpy as jnp
from jax import lax
import numpy as np

D_MODEL = 4096
BATCH = 2
SEQ = 4096
DEPTH = 1
DEC_BATCH = 4
DEC_SEQ = 4096
PAST_LEN = 128

N_META = 16
GRID_W = 64
Q_BLOCK = 128
HEAD_DIM = 128
MLA_HEADS = 16
MLA_Q_LORA = 1024
MLA_KV_LORA = 512
MLA_NOPE_DIM = 128
MLA_ROPE_DIM = 64
MLA_V_DIM = 128
GQA_HEADS = 16
GQA_KV_HEADS = 4
GQA_GROUP = GQA_HEADS // GQA_KV_HEADS
D_FF = 11008
CONV_WIDTH = 3
ROPE_THETA = 10000.0
NORM_EPS = 1e-6
IN_SPLITS = (MLA_Q_LORA, MLA_KV_LORA, MLA_ROPE_DIM, GQA_HEADS * HEAD_DIM, GQA_KV_HEADS * HEAD_DIM, GQA_KV_HEADS * HEAD_DIM, D_MODEL, D_MODEL)
D_IN = sum(IN_SPLITS)

kernel_name = "hybrid_mla_axial_gqa_encoder"


def _rms(x, g):
    x32 = x.astype(jnp.float32)
    y = x32 * lax.rsqrt(jnp.mean(x32 * x32, axis=-1, keepdims=True) + NORM_EPS)
    return (y * g.astype(jnp.float32)).astype(x.dtype)


def _rope_tables(pos, dim):
    inv = ROPE_THETA ** (-jnp.arange(0, dim, 2, dtype=jnp.float32) / dim)
    ang = pos.astype(jnp.float32)[:, None] * inv[None, :]
    return jnp.cos(ang), jnp.sin(ang)


def _apply_rope(x, cos, sin):
    half = x.shape[-1] // 2
    c = cos[None, :, None, :].astype(x.dtype)
    s = sin[None, :, None, :].astype(x.dtype)
    x1, x2 = x[..., :half], x[..., half:]
    return jnp.concatenate([x1 * c - x2 * s, x1 * s + x2 * c], axis=-1)


def _axial_rope(x, row_cos, row_sin, col_cos, col_sin):
    half = x.shape[-1] // 2
    return jnp.concatenate([_apply_rope(x[..., :half], row_cos, row_sin),
                            _apply_rope(x[..., half:], col_cos, col_sin)], axis=-1)


def _attend_block(q, k, v, scale):
    s = jnp.einsum("bqkgd,bskd->bkgqs", q, k).astype(jnp.float32) * scale
    p = jax.nn.softmax(s, axis=-1).astype(v.dtype)
    return jnp.einsum("bkgqs,bskd->bqkgd", p, v)


def _bidir_attention(q, k, v, scale):
    b, l, hk, g, d = q.shape
    n_real = l - N_META
    n_blk = n_real // Q_BLOCK
    o_meta = _attend_block(q[:, :N_META], k, v, scale)
    q_blocks = jnp.moveaxis(q[:, N_META:].reshape(b, n_blk, Q_BLOCK, hk, g, d), 1, 0)
    o_blocks = lax.map(lambda qb: _attend_block(qb, k, v, scale), q_blocks)
    o_real = jnp.moveaxis(o_blocks, 0, 1).reshape(b, n_real, hk, g, v.shape[-1])
    return jnp.concatenate([o_meta, o_real], axis=1)


def _dwconv3(a, w, bias):
    ap = jnp.pad(a, ((0, 0), (1, 1), (0, 0)))
    return ap[:, :-2] * w[0] + ap[:, 1:-1] * w[1] + ap[:, 2:] * w[2] + bias


def _layer(h, rope1, axial, g_pre_mix, w_in, g_cq, w_uq, g_ckv, w_ukv, g_qn, g_kn,
           w_pa, w_pb, w_o, g_post_mix, g_pre_ffn, w_up, w_conv, b_conv, w_down, g_post_ffn):
    b, l, _ = h.shape
    cos1, sin1 = rope1
    u = _rms(h, g_pre_mix)
    z = u @ w_in
    offsets = np.cumsum(IN_SPLITS)[:-1].tolist()
    z_cq, z_ckv, z_kr, z_gq, z_gk, z_gv, z_ga, z_gb = jnp.split(z, offsets, axis=-1)

    cq = _rms(z_cq, g_cq)
    qa = (cq @ w_uq).reshape(b, l, MLA_HEADS, MLA_NOPE_DIM + MLA_ROPE_DIM)
    qa = jnp.concatenate([qa[..., :MLA_NOPE_DIM], _apply_rope(qa[..., MLA_NOPE_DIM:], cos1, sin1)], axis=-1)
    ckv = _rms(z_ckv, g_ckv)
    kv = (ckv @ w_ukv).reshape(b, l, MLA_HEADS, MLA_NOPE_DIM + MLA_V_DIM)
    k_nope, v_a = kv[..., :MLA_NOPE_DIM], kv[..., MLA_NOPE_DIM:]
    k_rope = _apply_rope(z_kr[:, :, None, :], cos1, sin1)
    ka = jnp.concatenate([k_nope, jnp.broadcast_to(k_rope, (b, l, MLA_HEADS, MLA_ROPE_DIM))], axis=-1)
    o_a = _bidir_attention(qa[:, :, :, None, :], ka, v_a, 1.0 / math.sqrt(MLA_NOPE_DIM + MLA_ROPE_DIM))
    o_a = o_a.reshape(b, l, MLA_HEADS * MLA_V_DIM)

    qb = _axial_rope(_rms(z_gq.reshape(b, l, GQA_HEADS, HEAD_DIM), g_qn), *axial)
    kb = _axial_rope(_rms(z_gk.reshape(b, l, GQA_KV_HEADS, HEAD_DIM), g_kn), *axial)
    vb = z_gv.reshape(b, l, GQA_KV_HEADS, HEAD_DIM)
    qb = qb.reshape(b, l, GQA_KV_HEADS, GQA_GROUP, HEAD_DIM)
    o_b = _bidir_attention(qb, kb, vb, 1.0 / math.sqrt(HEAD_DIM)).reshape(b, l, GQA_HEADS * HEAD_DIM)

    merged = jax.nn.sigmoid(z_ga) * (o_a @ w_pa) + jax.nn.sigmoid(z_gb) * (o_b @ w_pb)
    h = h + _rms(merged @ w_o, g_post_mix)

    up = _rms(h, g_pre_ffn) @ w_up
    a, gv = up[..., :D_FF], up[..., D_FF:]
    f = jax.nn.gelu(_dwconv3(a, w_conv, b_conv)) * gv
    return h + _rms(f @ w_down, g_post_ffn)


def _trunk(x, meta_tokens, params):
    b, n_tok, _ = x.shape
    rows = n_tok // GRID_W
    h = jnp.concatenate([jnp.broadcast_to(meta_tokens.astype(x.dtype)[None], (b, N_META, D_MODEL)), x], axis=1)
    l = N_META + n_tok
    rope1 = _rope_tables(jnp.arange(l), MLA_ROPE_DIM)
    row_ids = jnp.concatenate([jnp.zeros((N_META,), jnp.int32), jnp.repeat(jnp.arange(rows, dtype=jnp.int32), GRID_W)])
    col_ids = jnp.concatenate([jnp.zeros((N_META,), jnp.int32), jnp.tile(jnp.arange(GRID_W, dtype=jnp.int32), rows)])
    row_cos, row_sin = _rope_tables(row_ids, HEAD_DIM // 2)
    col_cos, col_sin = _rope_tables(col_ids, HEAD_DIM // 2)
    axial = (row_cos, row_sin, col_cos, col_sin)
    for i in range(DEPTH):
        h = _layer(h, rope1, axial, *[p[i] for p in params])
    return h[:, N_META:]


def setup_inputs(seed: int = 0) -> dict:
    key = jax.random.key(seed)
    ks = jax.random.split(key, 24)
    f32 = jnp.float32

    def nrm(k, shape, scale):
        return jax.random.normal(k, shape, f32) * scale

    def gain(k, n):
        return 1.0 + 0.02 * jax.random.normal(k, (DEPTH, n), f32)

    return {
        "x_prompt": nrm(ks[0], (BATCH, SEQ, D_MODEL), 1.0),
        "x_sample": nrm(ks[1], (DEC_BATCH, DEC_SEQ, D_MODEL), 1.0),
        "meta_tokens": nrm(ks[2], (N_META, D_MODEL), 1.0),
        "g_pre_mix": gain(ks[3], D_MODEL),
        "w_in": nrm(ks[4], (DEPTH, D_MODEL, D_IN), D_MODEL ** -0.5),
        "g_cq": gain(ks[5], MLA_Q_LORA),
        "w_uq": nrm(ks[6], (DEPTH, MLA_Q_LORA, MLA_HEADS * (MLA_NOPE_DIM + MLA_ROPE_DIM)), MLA_Q_LORA ** -0.5),
        "g_ckv": gain(ks[7], MLA_KV_LORA),
        "w_ukv": nrm(ks[8], (DEPTH, MLA_KV_LORA, MLA_HEADS * (MLA_NOPE_DIM + MLA_V_DIM)), MLA_KV_LORA ** -0.5),
        "g_qn": gain(ks[9], HEAD_DIM),
        "g_kn": gain(ks[10], HEAD_DIM),
        "w_pa": nrm(ks[11], (DEPTH, MLA_HEADS * MLA_V_DIM, D_MODEL), (MLA_HEADS * MLA_V_DIM) ** -0.5),
        "w_pb": nrm(ks[12], (DEPTH, GQA_HEADS * HEAD_DIM, D_MODEL), (GQA_HEADS * HEAD_DIM) ** -0.5),
        "w_o": nrm(ks[13], (DEPTH, D_MODEL, D_MODEL), D_MODEL ** -0.5),
        "g_post_mix": gain(ks[14], D_MODEL),
        "g_pre_ffn": gain(ks[15], D_MODEL),
        "w_up": nrm(ks[16], (DEPTH, D_MODEL, 2 * D_FF), D_MODEL ** -0.5),
        "w_conv": nrm(ks[17], (DEPTH, CONV_WIDTH, D_FF), CONV_WIDTH ** -0.5),
        "b_conv": nrm(ks[18], (DEPTH, D_FF), 0.01),
        "w_down": nrm(ks[19], (DEPTH, D_FF, D_MODEL), D_FF ** -0.5),
        "g_post_ffn": gain(ks[20], D_MODEL),
    }


def reference(x_prompt, x_sample, meta_tokens, g_pre_mix, w_in, g_cq, w_uq, g_ckv, w_ukv, g_qn, g_kn,
              w_pa, w_pb, w_o, g_post_mix, g_pre_ffn, w_up, w_conv, b_conv, w_down, g_post_ffn):
    params = (g_pre_mix, w_in, g_cq, w_uq, g_ckv, w_ukv, g_qn, g_kn, w_pa, w_pb, w_o,
              g_post_mix, g_pre_ffn, w_up, w_conv, b_conv, w_down, g_post_ffn)
    y_prompt = _trunk(x_prompt, meta_tokens, params)
    y_sample = _trunk(x_sample, meta_tokens, params)
    return (y_prompt, y_sample)
```

```python
import contextlib
import math
import numpy as np
import concourse.bass as bass
import concourse.mybir as mybir
from concourse.bass_utils import run_bass_kernel_spmd

F32 = mybir.dt.float32
BF16 = mybir.dt.bfloat16
AF = mybir.ActivationFunctionType
ALU = mybir.AluOpType
EPS = 1e-6
N_META = 16
GRID_W = 64
ROPE_THETA = 10000.0


class Cfg:
    def __init__(self, D=4096, QL=1024, KVL=512, H=16, HQ=16, HKV=4, DFF=11008, SEQ=4096, T=512):
        self.D, self.QL, self.KVL, self.H, self.HQ, self.HKV, self.DFF, self.SEQ, self.T = D, QL, KVL, H, HQ, HKV, DFF, SEQ, T
        self.G = HQ // HKV
        self.DC, self.QC, self.KC, self.FC = D // 128, QL // 128, KVL // 128, DFF // 128
        self.L = SEQ + N_META
        self.NCH = SEQ // 128 + 1
        self.LP = self.NCH * 128
        self.NU = 3
        self.UNIT = SEQ // 4
        self.TPU = self.UNIT // T
        self.NT = self.NU * self.TPU
        self.NQ = self.NU * self.UNIT
        self.NH = 2 * self.NU
        self.NQP = self.NQ + 128
        self.in_splits = (QL, KVL, 64, HQ * 128, HKV * 128, HKV * 128, D, D)
        self.DIN = sum(self.in_splits)


FULL = Cfg()


def _kp(wcols, M):
    K, n = wcols.shape
    kc = K // 128
    nt = n // M
    a = wcols.reshape(kc, 128, nt, M).transpose(1, 2, 0, 3)
    return np.ascontiguousarray(a).reshape(128, nt * kc * M)


def _axial_perm():
    return np.concatenate([np.arange(0, 32), np.arange(64, 96), np.arange(32, 64), np.arange(96, 128)])


def blob_specs(c):
    return [
        ("ckv", c.KC, c.DC, 128), ("kr", 1, c.DC, 64), ("gk", c.HKV, c.DC, 128), ("gv", c.HKV, c.DC, 128),
        ("uk", c.H, c.KC, 128), ("uv", c.H, c.KC, 128),
        ("cq", c.QC, c.DC, 128), ("gq", c.HQ, c.DC, 128), ("uqn", c.H, c.QC, 128), ("uqr", c.H, c.QC, 64),
        ("ga", c.DC, c.DC, 128), ("pa", c.DC, c.H, 128), ("gb", c.DC, c.DC, 128), ("pb", c.DC, c.HQ, 128),
        ("wo", c.DC, c.DC, 128), ("upa", c.FC, c.DC, 128), ("upg", c.FC, c.DC, 128), ("dn", c.DC, c.FC, 128),
    ]


def blob_offsets(c):
    off, o = {}, 0
    for name, nt, kc, m in blob_specs(c):
        off[name] = (o, nt, kc, m)
        o += nt * kc * m
    return off, o


def build_wall(c, w_in, w_uq, w_ukv, w_pa, w_pb, w_o, w_up, w_down):
    offs = np.cumsum((0,) + c.in_splits)
    seg = lambda i: w_in[:, offs[i]:offs[i + 1]]
    perm = _axial_perm()
    hp = lambda w, nh: w.reshape(w.shape[0], nh, 128)[:, :, perm].reshape(w.shape[0], nh * 128)
    uq = w_uq.reshape(c.QL, c.H, 192)
    ukv = w_ukv.reshape(c.KVL, c.H, 256)
    parts = {
        "ckv": _kp(seg(1), 128), "kr": _kp(seg(2), 64), "gk": _kp(hp(seg(4), c.HKV), 128), "gv": _kp(seg(5), 128),
        "uk": _kp(ukv[:, :, :128].reshape(c.KVL, -1), 128), "uv": _kp(ukv[:, :, 128:].reshape(c.KVL, -1), 128),
        "cq": _kp(seg(0), 128), "gq": _kp(hp(seg(3), c.HQ), 128),
        "uqn": _kp(uq[:, :, :128].reshape(c.QL, -1), 128), "uqr": _kp(uq[:, :, 128:].reshape(c.QL, -1), 64),
        "ga": _kp(seg(6), 128), "pa": _kp(w_pa, 128), "gb": _kp(seg(7), 128), "pb": _kp(w_pb, 128),
        "wo": _kp(w_o, 128), "upa": _kp(w_up[:, :c.DFF], 128), "upg": _kp(w_up[:, c.DFF:], 128), "dn": _kp(w_down, 128),
    }
    return np.concatenate([parts[n] for n, *_ in blob_specs(c)], axis=1)


def kv_positions(c):
    pos = np.concatenate([np.arange(N_META, c.L), np.arange(N_META)]).astype(np.float32)
    rows = c.SEQ // GRID_W
    row_ids = np.concatenate([np.repeat(np.arange(rows), GRID_W), np.zeros(N_META)]).astype(np.float32)
    col_ids = np.concatenate([np.tile(np.arange(GRID_W), rows), np.zeros(N_META)]).astype(np.float32)
    return pos, row_ids, col_ids


def rope_tables(c, pos, row_ids, col_ids, width):
    def tab(p, dim):
        inv = (ROPE_THETA ** (-np.arange(0, dim, 2, dtype=np.float32) / np.float32(dim))).astype(np.float32)
        ang = p[None, :] * inv[:, None]
        return np.cos(ang).astype(np.float32), np.sin(ang).astype(np.float32)

    c1, s1 = tab(pos, 64)
    rc, rs = tab(row_ids, 64)
    cc, cs = tab(col_ids, 64)
    n_ = pos.shape[0]
    ta = np.zeros((128, 2, width), np.float32)
    ta[:64, 0, :n_] = np.concatenate([c1, c1]); ta[:64, 1, :n_] = np.concatenate([s1, -s1])
    tb = np.zeros((128, 2, width), np.float32)
    tb[:, 0, :n_] = np.concatenate([rc, cc, rc, cc]); tb[:, 1, :n_] = np.concatenate([rs, cs, -rs, -cs])
    return ta, tb


class Obj:
    __slots__ = ("name", "w", "r", "dsem", "dcnt", "psum")

    def __init__(self, name, psum=False):
        self.name, self.w, self.r, self.dsem, self.dcnt, self.psum = name, {}, {}, None, 0, psum


class KB:
    def __init__(self, nc, es):
        self.nc, self.es = nc, es
        self.eng = dict(pe=nc.tensor, act=nc.scalar, dve=nc.vector, pool=nc.gpsimd, sp=nc.sync)
        self.sems = []
        self.tl = {e: self.newsem("tl_" + e) for e in ("pe", "act", "dve", "pool")}
        self.cnt = {e: 0 for e in self.tl}
        self.waited = {e: {} for e in self.eng}
        self.pe_pending = []
        self.issued = {}
        self.named = {}

    def newsem(self, name):
        s = self.es.enter_context(self.nc.semaphore(name))
        self.sems.append(s)
        return len(self.sems) - 1

    def _wait(self, e, evs):
        for si, val in evs.items():
            if e == "pe" and si == self.tl["pe"]:
                continue
            if si in self.issued:
                val = self.issued[si]
            if self.waited[e].get(si, 0) >= val:
                continue
            self.eng[e].wait_ge(self.sems[si], val)
            self.waited[e][si] = val

    def op(self, e, fn, reads=(), writes=(), sig=True):
        for o in reads:
            self._wait(e, o.w)
            if o.psum:
                self._wait(e, {k: v for k, v in o.r.items() if k != self.tl.get(e)})
        for o in writes:
            self._wait(e, o.w)
            self._wait(e, o.r)
        ins = fn(self.eng[e])
        if sig:
            self.cnt[e] += 1
            ins.then_inc(self.sems[self.tl[e]], 1)
            si, v = self.tl[e], self.cnt[e]
            for o in reads:
                o.r[si] = v
            for o in writes:
                o.w[si] = v
            if e == "pe":
                for o in self.pe_pending:
                    o.r[si] = v
                self.pe_pending = []
        else:
            assert e == "pe"
            self.pe_pending.extend(reads)
        return ins

    def dma(self, q, out_ap, in_ap, reads=(), writes=(), semobj=None):
        for o in reads:
            self._wait(q, o.w)
        for o in writes:
            self._wait(q, o.w)
            self._wait(q, o.r)
        so = semobj
        if so.dsem is None:
            if so.name not in self.named:
                self.named[so.name] = self.newsem("d_" + so.name)
                self.issued[self.named[so.name]] = 0
            so.dsem = self.named[so.name]
        self.issued[so.dsem] += 16
        self.eng[q].dma_start(out=out_ap, in_=in_ap).then_inc(self.sems[so.dsem], 16)
        for o in reads:
            o.r[so.dsem] = self.issued[so.dsem]
        for o in writes:
            o.w[so.dsem] = self.issued[so.dsem]

    def inherit(self, new_objs, old_objs):
        u = {}
        for o in old_objs:
            for d in (o.w, o.r):
                for k, v in d.items():
                    u[k] = max(u.get(k, 0), v)
        for o in new_objs:
            for k, v in u.items():
                o.w[k] = max(o.w.get(k, 0), v)
                o.r[k] = max(o.r.get(k, 0), v)


DEBUG = False
LAST_STATS = None
PHASES = 4
KVSTOP = 99
FFNSTOP = 99
EPSTOP = 99


def build_program(c):
    nc = bass.Bass("TRN2", target_bir_lowering=False)
    T, D, DC, L, LP, NCH = c.T, c.D, c.DC, c.L, c.LP, c.NCH
    woff, XTOT = blob_offsets(c)
    dt_in = lambda n, s: nc.dram_tensor(n, s, F32, kind="ExternalInput").ap()
    x_kv = dt_in("x_kv", [2, L, D])
    x_q = dt_in("x_q", [c.NQ + c.NH, D])
    tqA = dt_in("tqA", [128, 2, c.NQP])
    tqB = dt_in("tqB", [128, 2, c.NQP])
    hmask_in = dt_in("hmask", [1, 8])
    wall = dt_in("wall", [128, XTOT])
    tabA = dt_in("tabA", [128, 2, LP])
    tabB = dt_in("tabB", [128, 2, LP])
    gcols = dt_in("gcols", [128, 2 * DC + c.QC + c.KC + 2 + 4 * c.FC])
    grows = dt_in("grows", [2, D])
    ident_in = dt_in("ident", [128, 128])
    y = nc.dram_tensor("y", [c.NQ, D], F32, kind="ExternalOutput").ap()
    wbf = {name: nc.dram_tensor("wbf_" + name, [128, nt_ * kc_ * m_], BF16).ap() for name, nt_, kc_, m_ in blob_specs(c)}
    kTa_ = [nc.dram_tensor("kTa%d" % i, [c.H, 128, LP], BF16).ap() for i in range(2)]
    krT_ = [nc.dram_tensor("krT%d" % i, [64, LP], BF16).ap() for i in range(2)]
    va_ = [nc.dram_tensor("va%d" % i, [c.H, LP, 128], BF16).ap() for i in range(2)]
    kTb_ = [nc.dram_tensor("kTb%d" % i, [c.HKV, 128, LP], BF16).ap() for i in range(2)]
    vb_ = [nc.dram_tensor("vb%d" % i, [c.HKV, LP, 128], BF16).ap() for i in range(2)]
    hscr = nc.dram_tensor("hscr", [c.NQ, D], F32).ap()
    u2scr = nc.dram_tensor("u2scr", [c.NT, 128, DC, T], BF16).ap()
    u2edge = nc.dram_tensor("u2edge", [c.NT, 128, DC, 2], BF16).ap()
    u2halo = nc.dram_tensor("u2halo", [128, DC, 8], BF16).ap()

    es = contextlib.ExitStack()
    with es:
        kb = KB(nc, es)
        sbt = lambda n, s, d: es.enter_context(nc.sbuf_tensor(n, s, d))
        ident_f = sbt("ident_f", [128, 128], F32)
        ident_b = sbt("ident_b", [128, 128], BF16)
        ones_f = sbt("ones_f", [128, 128], F32)
        ones_q = sbt("ones_q", [128, 128], BF16)
        ones_k = sbt("ones_k", [128, 128], BF16)
        ones_h = sbt("ones_h", [128, 128], BF16)
        NG = 2 * DC + c.QC + c.KC + 2 + 4 * c.FC
        gc = sbt("gc", [128, NG], F32)
        zed = sbt("zed", [128, DC * 2], BF16)
        o_const = Obj("const")
        G_PRE, G_FFN = 0, DC
        G_CQ = 2 * DC; G_CKV = G_CQ + c.QC; G_QN = G_CKV + c.KC; G_KN = G_QN + 1
        G_CW = G_KN + 1; G_CB = G_CW + 3 * c.FC
        kb.dma("sp", ident_f[:], ident_in[:], writes=[o_const], semobj=o_const)
        kb.dma("sp", gc[:], gcols[:], writes=[o_const], semobj=o_const)
        kb.op("dve", lambda e: e.tensor_copy(ident_b[:], ident_f[:]), reads=[o_const], writes=[o_const])
        kb.op("dve", lambda e: e.memset(ones_f[:], 1.0), writes=[o_const])
        kb.op("dve", lambda e: e.memset(ones_q[:], 1.0 / c.QL), writes=[o_const])
        kb.op("dve", lambda e: e.memset(ones_k[:], 1.0 / c.KVL), writes=[o_const])
        kb.op("dve", lambda e: e.memset(ones_h[:], 1.0 / 128), writes=[o_const])
        kb.op("dve", lambda e: e.memset(zed[:], 0.0), writes=[o_const])
        o_edge = Obj("u2edge")
        hmask = sbt("hmask_sb", [128, 8], F32)
        kb.dma("sp", hmask[:], hmask_in[0:1, :].broadcast_to([128, 8]), writes=[o_const], semobj=o_const)

        pieces = []
        cast_todo = []
        PW = 65536
        KVBLOBS = ("ckv", "kr", "gk", "gv", "uk", "uv")
        for name, nt_, kc_, m_ in blob_specs(c):
            g0 = woff[name][0]
            tot = nt_ * kc_ * m_
            for a in range(0, tot, PW):
                b = min(tot, a + PW)
                po = Obj("wp_%s_%d" % (name, a))
                pieces.append((name, a, b, po))
                cast_todo.append((name, g0, a, b, po))

        def issue_casts(k):
            for _ in range(min(k, len(cast_todo))):
                name, g0, a, b, po = cast_todo.pop(0)
                kb.dma("pool", wbf[name][:, a:b], wall[:, g0 + a:g0 + b], writes=[po], semobj=Obj("wcast_" + name))

        issue_casts(len(cast_todo))
        casts_per_tile = 0

        def wobjs(name, a, b):
            return [po for (pn, pa, pb, po) in pieces if pn == name and pa < b and a < pb]

        NSLOT = 3
        SLOT = 4096
        wslots = [sbt("wslot%d" % i, [128, SLOT], BF16) for i in range(NSLOT)]
        wso = [Obj("wslot%d" % i) for i in range(NSLOT)]
        wstate = {"n": 0}

        class WStream:
            def __init__(self, reqs):
                self.reqs, self.issued, self.got = reqs, 0, 0
                self.slot = {}

            def _issue(self):
                name, off, n = self.reqs[self.issued]
                s = wstate["n"] % NSLOT
                wstate["n"] += 1
                kb.dma("sp", wslots[s][:, 0:n], wbf[name][:, off:off + n], reads=wobjs(name, off, off + n), writes=[wso[s]], semobj=wso[s])
                self.slot[self.issued] = s
                self.issued += 1

            def get(self):
                while self.issued < len(self.reqs) and self.issued < self.got + NSLOT - 1:
                    self._issue()
                if self.issued <= self.got:
                    self._issue()
                s = self.slot[self.got]
                self.got += 1
                return wslots[s], wso[s]

        def tile_req(name, j, kc0=0, kc1=None):
            o, nt, kc, m = woff[name]
            kc1 = kc if kc1 is None else kc1
            return (name, (j * kc + kc0) * m, (kc1 - kc0) * m)

        banks = [es.enter_context(nc.psum_tensor("bank%d" % i, [128, 512], F32)) for i in range(8)]
        bobj = [Obj("bank%d" % i, psum=True) for i in range(8)]
        pst = {"n": 0}

        def pbank():
            i = pst["n"] % 6
            pst["n"] += 1
            return banks[i], bobj[i]

        ARENA = 148 * 1024
        arena = sbt("arena", [128, ARENA // 2], BF16)
        cur_objs = []

        class View:
            def __init__(self, off, nbytes, dt, name):
                assert off % 4 == 0 and off + nbytes <= ARENA, (name, off, nbytes)
                a = arena[:, off // 2:(off + nbytes) // 2]
                self.ap = a.bitcast(F32) if dt == F32 else a
                self.o = Obj(name)
                self.end = off + nbytes

        def stage_views(specs, carry=None):
            nonlocal cur_objs
            carry = carry or {}
            vs = {n: View(off, nb, dt, n) for (n, off, nb, dt) in specs}
            for n_, o_ in carry.items():
                vs[n_].o = o_
            kb.inherit([v.o for n_, v in vs.items() if n_ not in carry], cur_objs)
            cur_objs = [v.o for v in vs.values()]
            return vs

        rsA = sbt("rsA", [128, T], F32); rsB = sbt("rsB", [128, T], F32)
        o_rsA, o_rsB = Obj("rsA"), Obj("rsB")
        st1 = sbt("st1", [128, 8], F32); o_st1 = Obj("st1")
        tA = sbt("tA", [128, 2, T], F32); tB = sbt("tB", [128, 2, T], F32)
        o_tA, o_tB = Obj("tA"), Obj("tB")
        NTMP = 4
        tmpf = [sbt("tmpf%d" % i, [128, T], F32) for i in range(NTMP)]
        o_tmpf = [Obj("tmpf%d" % i) for i in range(NTMP)]
        tmpb = [sbt("tmpb%d" % i, [128, T], BF16) for i in range(4)]
        o_tmpb = [Obj("tmpb%d" % i) for i in range(4)]
        rr = {"f": 0, "b": 0}

        def tf():
            i = rr["f"] % NTMP; rr["f"] += 1
            return tmpf[i], o_tmpf[i]

        def tb_():
            i = rr["b"] % 4; rr["b"] += 1
            return tmpb[i], o_tmpb[i]

        def blocks(n):
            return [(i, min(128, n - i)) for i in range(0, n, 128)]

        def emit_xnormT(xrows, n, gcol0, xt, o_xt, xn, o_xn, uT, o_uT, junk, o_junk):
            bl = blocks(n)
            for bi, (t0, nt) in enumerate(bl):
                kb.dma("sp", xt[0:nt, bi % 2, :], xrows[t0:t0 + nt, :], writes=[o_xt[bi % 2]], semobj=o_xt[bi % 2])
                kb.op("act", lambda e: e.activation(junk[0:nt, :], xt[0:nt, bi % 2, :], AF.Square, accum_out=st1[0:nt, 0:1]),
                      reads=[o_xt[bi % 2]], writes=[o_junk, o_st1])
                kb.op("act", lambda e: e.activation(st1[0:nt, 1:2], st1[0:nt, 0:1], AF.Sqrt, bias=EPS, scale=1.0 / D),
                      reads=[o_st1], writes=[o_st1])
                kb.op("dve", lambda e: e.reciprocal(st1[0:nt, 2:3], st1[0:nt, 1:2]), reads=[o_st1], writes=[o_st1])
                kb.op("dve", lambda e: e.tensor_scalar(xn[0:nt, bi, :], xt[0:nt, bi % 2, :], st1[0:nt, 2:3], None, ALU.mult),
                      reads=[o_xt[bi % 2], o_st1], writes=[o_xn])
            for cch in range(DC):
                bk, bo = pbank()
                bkb = bk.bitcast(BF16)
                for bi, (t0, nt) in enumerate(bl):
                    kb.op("pe", lambda e: e.transpose(bkb[:, t0:t0 + nt], xn[0:nt, bi, cch * 128:(cch + 1) * 128], ident_b[0:nt, 0:nt]),
                          reads=[o_xn, o_const], writes=[bo], sig=(bi == len(bl) - 1))
                eng = "dve" if cch % 2 == 0 else "act"
                if eng == "dve":
                    kb.op("dve", lambda e: e.tensor_scalar(uT[:, cch, 0:n], bkb[:, 0:n], gc[:, gcol0 + cch:gcol0 + cch + 1], None, ALU.mult),
                          reads=[bo, o_const], writes=[o_uT])
                else:
                    kb.op("act", lambda e: e.activation(uT[:, cch, 0:n], bkb[:, 0:n], AF.Copy, scale=gc[:, gcol0 + cch:gcol0 + cch + 1]),
                          reads=[bo, o_const], writes=[o_uT])

        def fm_group(ws, name, j, act, o_act, kcn, n, M=128, bank=None, kcsplit=32):
            bk, bo = bank if bank is not None else pbank()
            for k0 in range(0, kcn, kcsplit):
                k1 = min(kcn, k0 + kcsplit)
                wt, wo_ = ws.get()
                for kc in range(k0, k1):
                    last = kc == k1 - 1
                    kb.op("pe", lambda e: e.matmul(bk[0:M, 0:n], wt[:, (kc - k0) * M:(kc - k0 + 1) * M], act[:, kc, 0:n],
                                                   start=(kc == 0), stop=(kc == kcn - 1)),
                          reads=[wo_, o_act], writes=[bo], sig=last)
            return bk, bo

        def reqs_for(name, js, kcsplit=32):
            o, nt, kc, m = woff[name]
            r = []
            for j in js:
                for k0 in range(0, kc, kcsplit):
                    r.append(tile_req(name, j, k0, min(kc, k0 + kcsplit)))
            return r

        def rstd_from_mean(bk, bo, M, n, dst, o_dst):
            kb.op("act", lambda e: e.activation(dst[0:M, 0:n], bk[0:M, 0:n], AF.Sqrt, bias=EPS, scale=1.0), reads=[bo], writes=[o_dst])
            kb.op("dve", lambda e: e.reciprocal(dst[0:M, 0:n], dst[0:M, 0:n]), reads=[o_dst], writes=[o_dst])

        def emit_rope(src, o_src, M, n, tab, o_tab, dst_ap, o_dst):
            hf = M // 2
            t1, o1 = tf()
            t2, o2 = tf()
            kb.op("dve", lambda e: e.tensor_tensor(t1[0:M, 0:n], src[0:M, 0:n], tab[0:M, 0, 0:n], ALU.mult), reads=[o_src, o_tab], writes=[o1])
            kb.op("dve", lambda e: e.tensor_tensor(t2[0:hf, 0:n], src[hf:M, 0:n], tab[hf:M, 1, 0:n], ALU.mult), reads=[o_src, o_tab], writes=[o2])
            kb.op("dve", lambda e: e.tensor_tensor(t2[hf:M, 0:n], src[0:hf, 0:n], tab[0:hf, 1, 0:n], ALU.mult), reads=[o_src, o_tab], writes=[o2])
            kb.op("dve", lambda e: e.tensor_tensor(dst_ap, t1[0:M, 0:n], t2[0:M, 0:n], ALU.add), reads=[o1, o2], writes=[o_dst])

        def emit_headnorm_rope(bk, bo, n, gcol, tab, o_tab, dst_ap, o_dst):
            sq, osq = tb_()
            kb.op("act", lambda e: e.activation(sq[:, 0:n], bk[:, 0:n], AF.Square), reads=[bo], writes=[osq])
            b2, bo2 = pbank()
            kb.op("pe", lambda e: e.matmul(b2[:, 0:n], ones_h[:], sq[:, 0:n], start=True, stop=True), reads=[osq, o_const], writes=[bo2])
            rs, ors = tf()
            rstd_from_mean(b2, bo2, 128, n, rs, ors)
            xg, oxg = tf()
            kb.op("dve", lambda e: e.scalar_tensor_tensor(xg[:, 0:n], bk[:, 0:n], gc[:, gcol:gcol + 1], rs[:, 0:n], ALU.mult, ALU.mult),
                  reads=[bo, ors, o_const], writes=[oxg])
            emit_rope(xg, oxg, 128, n, tab, o_tab, dst_ap, o_dst)

        def load_tabs(t0, n, q=False):
            kb.dma("sp", tA[:, :, 0:n], (tqA if q else tabA)[:, :, t0:t0 + n], writes=[o_tA], semobj=o_tA)
            kb.dma("sp", tB[:, :, 0:n], (tqB if q else tabB)[:, :, t0:t0 + n], writes=[o_tB], semobj=o_tB)

        o_kv = Obj("kvscratch")
        o_dbg = Obj("dbg")

        def dbg(name, view_ap, vobj, shape):
            if not DEBUG:
                return
            t = nc.dram_tensor("dbg_" + name, shape, BF16).ap()
            kb.dma("pool", t, view_ap, reads=[vobj], writes=[o_dbg], semobj=o_dbg)

        def kv_tile(sset, t0, n):
            kTa, krT, va, kTb, vb = kTa_[sset], krT_[sset], va_[sset], kTb_[sset], vb_[sset]
            issue_casts(casts_per_tile)
            V = stage_views([
                ("xt0", 0, D * 4, F32), ("xt1", D * 4, D * 4, F32), ("xn", 32768, 4 * D * 2, BF16), ("uT", 65536, DC * T * 2, BF16),
                ("junk", 98304, D * 2, BF16), ("zc", 106496, c.KC * T * 4, F32), ("ckvn", 106496 + c.KC * T * 4, c.KC * T * 2, BF16),
            ] + [("kst%d" % i, 122880 + i * T * 2, T * 2, BF16) for i in range(4)]
              + [("vst%d" % i, 122880 + 4 * T * 2 + i * 128 * 4 * 2, 128 * 4 * 2, BF16) for i in range(4)])
            xt = arena[:, 0:4 * D].bitcast(F32).rearrange("p (b d) -> p b d", b=2)
            o_xt = [V["xt0"].o, V["xt1"].o]
            xn = V["xn"].ap.rearrange("p (b d) -> p b d", b=4)
            uT = V["uT"].ap.rearrange("p (c t) -> p c t", c=DC)
            zc = V["zc"].ap.rearrange("p (c t) -> p c t", c=c.KC)
            ckvn = V["ckvn"].ap.rearrange("p (c t) -> p c t", c=c.KC)
            kst = arena[:, 122880 // 2:(122880 + 4 * T * 2) // 2].rearrange("p (h t) -> p h t", h=4)
            vst = arena[:, (122880 + 4 * T * 2) // 2:(122880 + 4 * T * 2 + 4 * 128 * 4 * 2) // 2].rearrange("p (h b d) -> p h b d", h=4, b=4)
            ksl = {"n": 0}

            def kslot():
                i = ksl["n"] % 4
                ksl["n"] += 1
                return i, V["kst%d" % i].o
            load_tabs(t0, n)
            emit_xnormT(x_kv[sset, t0:t0 + n, :], n, G_PRE, xt, o_xt, xn, V["xn"].o, uT, V["uT"].o, V["junk"].ap, V["junk"].o)
            o_uT = V["uT"].o
            if KVSTOP <= 1:
                return
            bl = blocks(n)
            ws = WStream(reqs_for("ckv", range(c.KC)) + reqs_for("kr", [0]) + reqs_for("gk", range(c.HKV)) + reqs_for("gv", range(c.HKV))
                         + reqs_for("uk", range(c.H)) + reqs_for("uv", range(c.H)))
            sqs = []
            bS, boS = banks[7], bobj[7]
            for j in range(c.KC):
                bk, bo = fm_group(ws, "ckv", j, uT, o_uT, DC, n)
                if KVSTOP <= 1.2:
                    continue
                kb.op("dve", lambda e: e.tensor_copy(zc[:, j, 0:n], bk[:, 0:n]), reads=[bo], writes=[V["zc"].o])
                sq, osq = tb_()
                kb.op("act", lambda e: e.activation(sq[:, 0:n], bk[:, 0:n], AF.Square), reads=[bo], writes=[osq])
                if KVSTOP <= 1.5:
                    continue
                kb.op("pe", lambda e: e.matmul(bS[:, 0:n], ones_k[:], sq[:, 0:n], start=(j == 0), stop=(j == c.KC - 1)),
                      reads=[osq, o_const], writes=[boS], sig=True)
            if KVSTOP <= 1.7:
                return
            rstd_from_mean(bS, boS, 128, n, rsA, o_rsA)
            if KVSTOP <= 1.9:
                return
            for j in range(c.KC):
                kb.op("dve", lambda e: e.scalar_tensor_tensor(ckvn[:, j, 0:n], zc[:, j, 0:n], gc[:, G_CKV + j:G_CKV + j + 1], rsA[:, 0:n], ALU.mult, ALU.mult),
                      reads=[V["zc"].o, o_rsA, o_const], writes=[V["ckvn"].o])
            if KVSTOP <= 2:
                return
            bk, bo = fm_group(ws, "kr", 0, uT, o_uT, DC, n, M=64)
            xr, oxr = tf()
            kb.op("dve", lambda e: e.tensor_copy(xr[0:64, 0:n], bk[0:64, 0:n]), reads=[bo], writes=[oxr])
            ki, ko = kslot()
            emit_rope(xr, oxr, 64, n, tA, o_tA, kst[0:64, ki, 0:n], ko)
            kb.dma("pool", krT[:, t0:t0 + n], kst[0:64, ki, 0:n], reads=[ko], writes=[o_kv], semobj=ko)
            if KVSTOP <= 3:
                return
            for h in range(c.HKV):
                bk, bo = fm_group(ws, "gk", h, uT, o_uT, DC, n)
                ki, ko = kslot()
                emit_headnorm_rope(bk, bo, n, G_KN, tB, o_tB, kst[:, ki, 0:n], ko)
                kb.dma("pool", kTb[h, :, t0:t0 + n], kst[:, ki, 0:n], reads=[ko], writes=[o_kv], semobj=ko)

            if KVSTOP <= 4:
                return

            def emit_v(ws_, name, h, act, o_act, kcn, dst):
                bk, bo = fm_group(ws_, name, h, act, o_act, kcn, n)
                vt, ovt = tb_()
                kb.op("act", lambda e: e.activation(vt[:, 0:n], bk[:, 0:n], AF.Copy), reads=[bo], writes=[ovt])
                b2, bo2 = pbank()
                b2b = b2.bitcast(BF16)
                for bi, (s0, nt) in enumerate(bl):
                    kb.op("pe", lambda e: e.transpose(b2b[0:nt, bi * 128:(bi + 1) * 128], vt[:, s0:s0 + nt], ident_b[:, :]),
                          reads=[ovt, o_const], writes=[bo2], sig=(bi == len(bl) - 1))
                nb = len(bl)
                vi = ksl["n"] % 4
                ksl["n"] += 1
                vo = V["vst%d" % vi].o
                if n % 128 == 0:
                    kb.op("dve", lambda e: e.tensor_copy(vst[:, vi, 0:nb, :], b2b[:, 0:nb * 128].rearrange("p (b d) -> p b d", b=nb)),
                          reads=[bo2], writes=[vo])
                    kb.dma("pool", dst[h, t0:t0 + n, :].rearrange("(b p) d -> p b d", p=128), vst[:, vi, 0:nb, :],
                           reads=[vo], writes=[o_kv], semobj=vo)
                else:
                    kb.op("dve", lambda e: e.tensor_copy(vst[0:n, vi, 0, :], b2b[0:n, 0:128]), reads=[bo2], writes=[vo])
                    kb.dma("pool", dst[h, t0:t0 + n, :], vst[0:n, vi, 0, :], reads=[vo], writes=[o_kv], semobj=vo)

            for h in range(c.HKV):
                emit_v(ws, "gv", h, uT, o_uT, DC, vb)
            if KVSTOP <= 5:
                return
            for h in range(c.H):
                bk, bo = fm_group(ws, "uk", h, ckvn, V["ckvn"].o, c.KC, n)
                ki, ko = kslot()
                kb.op("act", lambda e: e.activation(kst[:, ki, 0:n], bk[:, 0:n], AF.Copy), reads=[bo], writes=[ko])
                kb.dma("pool", kTa[h, :, t0:t0 + n], kst[:, ki, 0:n], reads=[ko], writes=[o_kv], semobj=ko)
            for h in range(c.H):
                emit_v(ws, "uv", h, ckvn, V["ckvn"].o, c.KC, va)

        if PHASES >= 2:
            for sset in range(2):
                for i in range(c.SEQ // T):
                    kv_tile(sset, i * T, T)
                kv_tile(sset, c.SEQ, N_META)

        issue_casts(len(cast_todo))
        o_h = Obj("hscr"); o_u2 = Obj("u2scr")
        HCH = (NCH + 1) // 2
        KVB = HCH * 128 * 2

        def attn_tile(t0, n, ti):
            bl = blocks(n)
            nb = len(bl)
            V = stage_views([
                ("xt0", 0, D * 4, F32), ("xt1", D * 4, D * 4, F32), ("xn", 32768, 4 * D * 2, BF16), ("uT", 65536, DC * T * 2, BF16),
                ("junk", 98304, D * 2, BF16),
                ("qan", 106496, 0, BF16),
            ][:5])
            xt = arena[:, 0:4 * D].bitcast(F32).rearrange("p (b d) -> p b d", b=2)
            o_xt = [V["xt0"].o, V["xt1"].o]
            xn = V["xn"].ap.rearrange("p (b d) -> p b d", b=4)
            uT = V["uT"].ap.rearrange("p (c t) -> p c t", c=DC)
            o_uT = V["uT"].o
            load_tabs(t0, n, q=True)
            emit_xnormT(x_q[t0:t0 + n, :], n, G_PRE, xt, o_xt, xn, V["xn"].o, uT, o_uT, V["junk"].ap, V["junk"].o)
            QB = 65536 + max(DC, c.H + c.HQ) * T * 2
            V2 = stage_views([
                ("uT", 65536, DC * T * 2, BF16),
                ("zq", 0, c.QC * T * 4, F32), ("cqn", c.QC * T * 4, c.QC * T * 2, BF16),
                ("qan", QB, c.H * T * 2, BF16), ("qar", QB + c.H * T * 2, c.H * T * 2, BF16), ("qb", QB + 2 * c.H * T * 2, c.HQ * T * 2, BF16),
            ], carry={"uT": o_uT})
            zq = V2["zq"].ap.rearrange("p (c t) -> p c t", c=c.QC)
            cqn = V2["cqn"].ap.rearrange("p (c t) -> p c t", c=c.QC)
            qan = V2["qan"].ap.rearrange("p (h t) -> p h t", h=c.H)
            qar = V2["qar"].ap.rearrange("p (h t) -> p h t", h=c.H)
            qb = V2["qb"].ap.rearrange("p (h t) -> p h t", h=c.HQ)
            ws = WStream(reqs_for("cq", range(c.QC)) + reqs_for("gq", range(c.HQ)) + reqs_for("uqn", range(c.H)) + reqs_for("uqr", range(c.H)))
            bS, boS = banks[7], bobj[7]
            for j in range(c.QC):
                bk, bo = fm_group(ws, "cq", j, uT, o_uT, DC, n)
                kb.op("dve", lambda e: e.tensor_copy(zq[:, j, 0:n], bk[:, 0:n]), reads=[bo], writes=[V2["zq"].o])
                sq, osq = tb_()
                kb.op("act", lambda e: e.activation(sq[:, 0:n], bk[:, 0:n], AF.Square), reads=[bo], writes=[osq])
                kb.op("pe", lambda e: e.matmul(bS[:, 0:n], ones_q[:], sq[:, 0:n], start=(j == 0), stop=(j == c.QC - 1)),
                      reads=[osq, o_const], writes=[boS], sig=True)
            rstd_from_mean(bS, boS, 128, n, rsA, o_rsA)
            for j in range(c.QC):
                kb.op("dve", lambda e: e.scalar_tensor_tensor(cqn[:, j, 0:n], zq[:, j, 0:n], gc[:, G_CQ + j:G_CQ + j + 1], rsA[:, 0:n], ALU.mult, ALU.mult),
                      reads=[V2["zq"].o, o_rsA, o_const], writes=[V2["cqn"].o])
            for h in range(c.HQ):
                bk, bo = fm_group(ws, "gq", h, uT, o_uT, DC, n)
                emit_headnorm_rope(bk, bo, n, G_QN, tB, o_tB, qb[:, h, 0:n], V2["qb"].o)
            for h in range(c.H):
                bk, bo = fm_group(ws, "uqn", h, cqn, V2["cqn"].o, c.QC, n)
                kb.op("act", lambda e: e.activation(qan[:, h, 0:n], bk[:, 0:n], AF.Copy), reads=[bo], writes=[V2["qan"].o])
            for h in range(c.H):
                bk, bo = fm_group(ws, "uqr", h, cqn, V2["cqn"].o, c.QC, n, M=64)
                xr, oxr = tf()
                kb.op("dve", lambda e: e.tensor_copy(xr[0:64, 0:n], bk[0:64, 0:n]), reads=[bo], writes=[oxr])
                emit_rope(xr, oxr, 64, n, tA, o_tA, qar[0:64, h, 0:n], V2["qar"].o)
            specs = [("qan", QB, c.H * T * 2, BF16), ("qar", QB + c.H * T * 2, c.H * T * 2, BF16), ("qb", QB + 2 * c.H * T * 2, c.HQ * T * 2, BF16),
                     ("oa", 65536, c.H * T * 2, BF16), ("ob", 65536 + c.H * T * 2, c.HQ * T * 2, BF16),
                     ("kr", 0, LP * 2, BF16)]
            base = LP * 2
            for i in range(2):
                specs.append(("kh%d" % i, base + i * 2 * KVB, KVB, BF16))
                specs.append(("vh%d" % i, base + i * 2 * KVB + KVB, KVB, BF16))
            PB = base + 4 * KVB
            for i in range(4):
                specs.append(("pt%d" % i, PB + i * T * 2, T * 2, BF16))
            AB = PB + 4 * T * 2
            for i in range(2):
                specs.append(("acc%de" % i, AB + i * 2 * T * 4, T * 4, F32)); specs.append(("acc%do" % i, AB + i * 2 * T * 4 + T * 4, T * 4, F32))
            specs.append(("rcp", AB + 4 * T * 4, T * 4, F32))
            assert AB + 5 * T * 4 <= 65536
            V3 = stage_views(specs, carry={k_: V2[k_].o for k_ in ("qan", "qar", "qb")})
            oa = V3["oa"].ap.rearrange("p (h t) -> p h t", h=c.H)
            ob = V3["ob"].ap.rearrange("p (h t) -> p h t", h=c.HQ)
            kr_sb = V3["kr"].ap
            hb = {"n": 0, "o": 0}
            if ti >= 0:
                passes = [(0 if ti < c.TPU else 1, 0, n)]
            else:
                passes = [(0, 0, 2), (1, 2, n)]

            def attend(kT_src, v_src, q_list, scale, rope, n):
                nq = len(q_list)
                assert nq <= 2
                if nq == 2 and hb["o"] % 2 == 1:
                    hb["o"] += 1
                accb, obank = [], []
                for qi in range(nq):
                    k_ = (hb["o"] + qi) % 2
                    accb.append([(V3["acc0e"], V3["acc0o"]), (V3["acc1e"], V3["acc1o"])][k_])
                    obank.append((banks[6 + k_], bobj[6 + k_]))
                hb["o"] += nq
                for half in range(2):
                    c0, c1 = half * HCH, min(NCH, (half + 1) * HCH)
                    i = hb["n"] % 2
                    hb["n"] += 1
                    kh, vh = V3["kh%d" % i], V3["vh%d" % i]
                    k0 = c0 * 128
                    nk = (c1 - c0) * 128
                    kb.dma("sp", kh.ap[:, 0:nk], kT_src[:, k0:k0 + nk], reads=[o_kv], writes=[kh.o], semobj=kh.o)
                    nfull = min(c1, NCH - 1) - c0
                    vv = vh.ap.rearrange("p (c d) -> p c d", d=128)
                    if nfull > 0:
                        kb.dma("sp", vv[:, 0:nfull, :], v_src[k0:k0 + nfull * 128, :].rearrange("(c p) d -> p c d", p=128),
                               reads=[o_kv], writes=[vh.o], semobj=vh.o)
                    if c1 == NCH:
                        kb.dma("sp", vv[0:N_META, nfull, :], v_src[c.SEQ:c.SEQ + N_META, :], reads=[o_kv], writes=[vh.o], semobj=vh.o)
                    for qi, (qn_ap, qr_ap, oq, dst_ap, o_dst) in enumerate(q_list):
                        ob_k, ob_o = obank[qi]
                        acc = accb[qi]
                        chunks = list(range(c0, c1))
                        pend = []

                        def qk(cc):
                            nkc = 128 if cc < NCH - 1 else N_META
                            sb_, so_ = pbank()
                            lo = (cc - c0) * 128
                            kb.op("pe", lambda e: e.matmul(sb_[0:nkc, 0:n], kh.ap[:, lo:lo + nkc], qn_ap, start=True, stop=not rope),
                                  reads=[kh.o] + oq, writes=[so_], sig=not rope)
                            if rope:
                                kb.op("pe", lambda e: e.matmul(sb_[0:nkc, 0:n], kr_sb[0:64, cc * 128:cc * 128 + nkc], qr_ap, start=False, stop=True),
                                      reads=[V3["kr"].o] + oq, writes=[so_], sig=True)
                            return (cc, nkc, sb_, so_)

                        def pv(item):
                            cc, nkc, sb_, so_ = item
                            pi = rr["b"] % 4
                            rr["b"] += 1
                            pt = V3["pt%d" % pi]
                            kb.op("act", lambda e: e.activation(pt.ap[0:nkc, 0:n], sb_[0:nkc, 0:n], AF.Exp, scale=scale), reads=[so_], writes=[pt.o])
                            kb.op("pe", lambda e: e.matmul(ob_k[:, 0:n], vv[0:nkc, cc - c0, :], pt.ap[0:nkc, 0:n], start=(cc == 0), stop=(cc == NCH - 1)),
                                  reads=[vh.o, pt.o], writes=[ob_o], sig=True)
                            ac = acc[cc % 2]
                            en = "pool" if cc % 2 == 0 else "dve"
                            if cc < 2:
                                kb.op(en, lambda e: e.tensor_copy(ac.ap[:, 0:n], pt.ap[:, 0:n]), reads=[pt.o], writes=[ac.o])
                            else:
                                kb.op(en, lambda e: e.tensor_tensor(ac.ap[0:nkc, 0:n], ac.ap[0:nkc, 0:n], pt.ap[0:nkc, 0:n], ALU.add),
                                      reads=[pt.o, ac.o], writes=[ac.o])

                        for cc in chunks:
                            pend.append(qk(cc))
                            if len(pend) > 2:
                                pv(pend.pop(0))
                        while pend:
                            pv(pend.pop(0))
                        if half == 1:
                            db, dbo = pbank()
                            kb.op("pe", lambda e: e.matmul(db[:, 0:n], ones_f[:], acc[0].ap[:, 0:n], start=True, stop=False),
                                  reads=[acc[0].o, o_const], writes=[dbo], sig=False)
                            kb.op("pe", lambda e: e.matmul(db[:, 0:n], ones_f[:], acc[1].ap[:, 0:n], start=False, stop=True),
                                  reads=[acc[1].o, o_const], writes=[dbo])
                            rcp = V3["rcp"]
                            kb.op("dve", lambda e: e.reciprocal(rcp.ap[:, 0:n], db[:, 0:n]), reads=[dbo], writes=[rcp.o])
                            kb.op("dve", lambda e: e.tensor_tensor(dst_ap, ob_k[:, 0:n], rcp.ap[:, 0:n], ALU.mult),
                                  reads=[ob_o, rcp.o], writes=[o_dst])

            if ti == 0:
                dbg("qan", qan, V3["qan"].o, [128, c.H, T]); dbg("qar", qar[0:64], V3["qar"].o, [64, c.H, T]); dbg("qb", qb, V3["qb"].o, [128, c.HQ, T])
            sA = 1.0 / math.sqrt(192.0)
            sB = 1.0 / math.sqrt(128.0)
            for (sset, q0, q1) in passes:
                kTa, krT, va, kTb, vb = kTa_[sset], krT_[sset], va_[sset], kTb_[sset], vb_[sset]
                kb.dma("sp", kr_sb[0:64, :], krT[:, :], reads=[o_kv], writes=[V3["kr"].o], semobj=V3["kr"].o)
                for h in range(c.H):
                    attend(kTa[h], va[h], [(qan[:, h, q0:q1], qar[0:64, h, q0:q1], [V3["qan"].o, V3["qar"].o], oa[:, h, q0:q1], V3["oa"].o)], sA, True, q1 - q0)
                for hk in range(c.HKV):
                    for g0 in range(0, c.G, 2):
                        ql = []
                        for g in range(g0, min(c.G, g0 + 2)):
                            h = hk * c.G + g
                            ql.append((qb[:, h, q0:q1], None, [V3["qb"].o], ob[:, h, q0:q1], V3["ob"].o))
                        attend(kTb[hk], vb[hk], ql, sB, False, q1 - q0)
            if ti == 0:
                dbg("oa", oa, V3["oa"].o, [128, c.H, T]); dbg("ob", ob, V3["ob"].o, [128, c.HQ, T])
            V4 = stage_views([
                ("oa", 65536, c.H * T * 2, BF16), ("ob", 65536 + c.H * T * 2, c.HQ * T * 2, BF16),
                ("xt0", 0, D * 4, F32), ("xt1", D * 4, D * 4, F32), ("xn", 32768, 4 * D * 2, BF16),
                ("uT", QB, DC * T * 2, BF16), ("junk", QB + DC * T * 2, D * 2, BF16),
            ], carry={"oa": V3["oa"].o, "ob": V3["ob"].o})
            o_xt = [V4["xt0"].o, V4["xt1"].o]
            uT2 = V4["uT"].ap.rearrange("p (c t) -> p c t", c=DC)
            emit_xnormT(x_q[t0:t0 + n, :], n, G_PRE, xt, o_xt, xn, V4["xn"].o, uT2, V4["uT"].o, V4["junk"].ap, V4["junk"].o)
            V5 = stage_views([
                ("oa", 65536, c.H * T * 2, BF16), ("ob", 65536 + c.H * T * 2, c.HQ * T * 2, BF16),
                ("uT", QB, DC * T * 2, BF16), ("mg", 0, DC * T * 2, BF16),
            ], carry={"oa": V3["oa"].o, "ob": V3["ob"].o, "uT": V4["uT"].o})
            mg = V5["mg"].ap.rearrange("p (c t) -> p c t", c=DC)
            rq = []
            for j in range(DC):
                rq += reqs_for("ga", [j]) + reqs_for("pa", [j]) + reqs_for("gb", [j]) + reqs_for("pb", [j])
            ws = WStream(rq)
            for j in range(DC):
                bg, bgo = fm_group(ws, "ga", j, uT2, V5["uT"].o, DC, n)
                ba, bao = fm_group(ws, "pa", j, oa, V5["oa"].o, c.H, n)
                s1, os1 = tf()
                kb.op("act", lambda e: e.activation(s1[:, 0:n], bg[:, 0:n], AF.Sigmoid), reads=[bgo], writes=[os1])
                m1, om1 = tf()
                kb.op("dve", lambda e: e.tensor_tensor(m1[:, 0:n], ba[:, 0:n], s1[:, 0:n], ALU.mult), reads=[bao, os1], writes=[om1])
                bg2, bgo2 = fm_group(ws, "gb", j, uT2, V5["uT"].o, DC, n)
                bb, bbo = fm_group(ws, "pb", j, ob, V5["ob"].o, c.HQ, n)
                s2, os2 = tf()
                kb.op("act", lambda e: e.activation(s2[:, 0:n], bg2[:, 0:n], AF.Sigmoid), reads=[bgo2], writes=[os2])
                m2, om2 = tf()
                kb.op("dve", lambda e: e.tensor_tensor(m2[:, 0:n], bb[:, 0:n], s2[:, 0:n], ALU.mult), reads=[bbo, os2], writes=[om2])
                kb.op("pool", lambda e: e.tensor_tensor(mg[:, j, 0:n], m1[:, 0:n], m2[:, 0:n], ALU.add), reads=[om1, om2], writes=[V5["mg"].o])
            if ti == 0:
                dbg("mg", mg, V5["mg"].o, [128, DC, T])
            V6 = stage_views([
                ("mg", 0, DC * T * 2, BF16), ("dT", 32768, DC * T * 2, BF16),
                ("drow", 65536, D * 4, F32), ("xrow", 65536 + D * 4, D * 4, F32), ("hn", 65536 + 2 * D * 4, D * 2, BF16),
                ("junk", 65536 + 2 * D * 4 + D * 2, D * 2, BF16), ("u2T", 65536 + 2 * D * 4 + 2 * D * 2, DC * T * 2, BF16),
                ("edge", 65536 + 2 * D * 4 + 2 * D * 2 + DC * T * 2, DC * 2 * 2, BF16),
            ], carry={"mg": V5["mg"].o})
            dT = V6["dT"].ap.rearrange("p (c t) -> p c t", c=DC)
            ws = WStream(reqs_for("wo", range(DC)))
            for j in range(DC):
                bk, bo = fm_group(ws, "wo", j, mg, V6["mg"].o, DC, n)
                kb.op("act" if j % 2 else "dve", (lambda e: e.activation(dT[:, j, 0:n], bk[:, 0:n], AF.Copy)) if j % 2 else
                      (lambda e: e.tensor_copy(dT[:, j, 0:n], bk[:, 0:n])), reads=[bo], writes=[V6["dT"].o])
            if ti == 0:
                dbg("dT", dT, V6["dT"].o, [128, DC, T])
            V7 = stage_views([
                ("grow", 0, D * 4, F32), ("dT", 32768, DC * T * 2, BF16),
                ("drow", 65536, D * 4, F32), ("xrow", 65536 + D * 4, D * 4, F32), ("hn", 65536 + 2 * D * 4, D * 2, BF16),
                ("junk", 65536 + 2 * D * 4 + D * 2, D * 2, BF16), ("u2T", 65536 + 2 * D * 4 + 2 * D * 2, DC * T * 2, BF16),
                ("edge", 65536 + 2 * D * 4 + 2 * D * 2 + DC * T * 2, DC * 2 * 2, BF16),
            ], carry={"dT": V6["dT"].o})
            emit_epilogue(V7, dT, n, t0, ti, bl, first=True)

        def emit_epilogue(V6, dT, n, t0, ti, bl, first):
            drow, xrow = V6["drow"], V6["xrow"]
            grow_row = 0 if first else 1
            grow, o_grow = V6["grow"].ap, V6["grow"].o
            kb.dma("sp", grow[:, :], grows[grow_row:grow_row + 1, :].broadcast_to([128, D]), writes=[o_grow], semobj=o_grow)
            src = x_q if first else hscr
            if not first:
                kb_reads = [o_h]
            else:
                kb_reads = []
            u2T = V6["u2T"].ap.rearrange("p (c t) -> p c t", c=DC) if first else None
            for bi, (s0, nt) in enumerate(bl):
                kb.dma("sp", xrow.ap[0:nt, :], src[t0 + s0:t0 + s0 + nt, :], reads=kb_reads, writes=[xrow.o], semobj=xrow.o)
                for c4 in range(0, DC, 4):
                    bk, bo = pbank()
                    bkb = bk.bitcast(BF16)
                    for cc in range(c4, min(DC, c4 + 4)):
                        kb.op("pe", lambda e: e.transpose(bkb[0:nt, (cc - c4) * 128:(cc - c4 + 1) * 128], dT[:, cc, s0:s0 + nt], ident_b[:, :]),
                              reads=[V6["dT"].o, o_const], writes=[bo], sig=(cc == min(DC, c4 + 4) - 1))
                    w = (min(DC, c4 + 4) - c4) * 128
                    kb.op("act" if (c4 // 4) % 2 else "dve",
                          (lambda e: e.activation(drow.ap[0:nt, c4 * 128:c4 * 128 + w], bkb[0:nt, 0:w], AF.Copy)) if (c4 // 4) % 2 else
                          (lambda e: e.tensor_copy(drow.ap[0:nt, c4 * 128:c4 * 128 + w], bkb[0:nt, 0:w])), reads=[bo], writes=[drow.o])
                if not first and EPSTOP <= 1:
                    continue
                kb.op("act", lambda e: e.activation(V6["junk"].ap[0:nt, :], drow.ap[0:nt, :], AF.Square, accum_out=st1[0:nt, 0:1]),
                      reads=[drow.o], writes=[V6["junk"].o, o_st1])
                kb.op("act", lambda e: e.activation(st1[0:nt, 1:2], st1[0:nt, 0:1], AF.Sqrt, bias=EPS, scale=1.0 / D), reads=[o_st1], writes=[o_st1])
                kb.op("dve", lambda e: e.reciprocal(st1[0:nt, 2:3], st1[0:nt, 1:2]), reads=[o_st1], writes=[o_st1])
                if not first and EPSTOP <= 2:
                    continue
                kb.op("dve", lambda e: e.scalar_tensor_tensor(drow.ap[0:nt, :], drow.ap[0:nt, :], st1[0:nt, 2:3], grow[0:nt, :], ALU.mult, ALU.mult),
                      reads=[drow.o, o_st1, o_grow], writes=[drow.o])
                kb.op("pool", lambda e: e.tensor_tensor(drow.ap[0:nt, :], drow.ap[0:nt, :], xrow.ap[0:nt, :], ALU.add),
                      reads=[drow.o, xrow.o], writes=[drow.o])
                if not first:
                    if EPSTOP > 3:
                        kb.dma("sp", y[t0 + s0:t0 + s0 + nt, :], drow.ap[0:nt, :], reads=[drow.o], writes=[o_y], semobj=drow.o)
                    continue
                if ti >= 0:
                    kb.dma("pool", hscr[t0 + s0:t0 + s0 + nt, :], drow.ap[0:nt, :], reads=[drow.o], writes=[o_h], semobj=drow.o)
                hn = V6["hn"]
                kb.op("act", lambda e: e.activation(V6["junk"].ap[0:nt, :], drow.ap[0:nt, :], AF.Square, accum_out=st1[0:nt, 3:4]),
                      reads=[drow.o], writes=[V6["junk"].o, o_st1])
                kb.op("act", lambda e: e.activation(st1[0:nt, 4:5], st1[0:nt, 3:4], AF.Sqrt, bias=EPS, scale=1.0 / D), reads=[o_st1], writes=[o_st1])
                kb.op("dve", lambda e: e.reciprocal(st1[0:nt, 5:6], st1[0:nt, 4:5]), reads=[o_st1], writes=[o_st1])
                kb.op("dve", lambda e: e.tensor_scalar(hn.ap[0:nt, :], drow.ap[0:nt, :], st1[0:nt, 5:6], None, ALU.mult),
                      reads=[drow.o, o_st1], writes=[hn.o])
                for c4 in range(0, DC, 4):
                    bk, bo = pbank()
                    bkb = bk.bitcast(BF16)
                    ce = min(DC, c4 + 4)
                    for cc in range(c4, ce):
                        kb.op("pe", lambda e: e.transpose(bkb[:, (cc - c4) * 128:(cc - c4) * 128 + nt], hn.ap[0:nt, cc * 128:(cc + 1) * 128], ident_b[0:nt, 0:nt]),
                              reads=[hn.o, o_const], writes=[bo], sig=(cc == ce - 1))
                    for cc in range(c4, ce):
                        kb.op("dve" if cc % 2 else "pool" if False else "dve",
                              lambda e: e.tensor_scalar(u2T[:, cc, s0:s0 + nt], bkb[:, (cc - c4) * 128:(cc - c4) * 128 + nt], gc[:, G_FFN + cc:G_FFN + cc + 1], None, ALU.mult),
                              reads=[bo, o_const], writes=[V6["u2T"].o])
            if first and ti >= 0:
                edge = V6["edge"].ap.rearrange("p (c t) -> p c t", t=2)
                kb.op("dve", lambda e: e.tensor_copy(edge[:, :, 0:1], u2T[:, :, 0:1]), reads=[V6["u2T"].o], writes=[V6["edge"].o])
                kb.op("dve", lambda e: e.tensor_copy(edge[:, :, 1:2], u2T[:, :, n - 1:n]), reads=[V6["u2T"].o], writes=[V6["edge"].o])
                kb.dma("pool", u2edge[ti], edge, reads=[V6["edge"].o], writes=[o_edge], semobj=V6["edge"].o)
                kb.dma("pool", u2scr[ti], u2T, reads=[V6["u2T"].o], writes=[o_u2], semobj=V6["u2T"].o)
            if first and ti < 0:
                for cc in range(DC):
                    kb.op("dve", lambda e: e.tensor_tensor(u2T[:, cc, 0:n], u2T[:, cc, 0:n], hmask[:, 0:n], ALU.mult),
                          reads=[V6["u2T"].o, o_const], writes=[V6["u2T"].o])
                kb.dma("pool", u2halo[:, :, 0:n], u2T[:, :, 0:n], reads=[V6["u2T"].o], writes=[o_edge], semobj=V6["u2T"].o)

        o_y = Obj("y")
        if PHASES >= 3:
            attn_tile(c.NQ, c.NH, -1)
            for i in range(c.NT):
                attn_tile(i * T, T, i)

        def ffn_tile(ti):
            t0, n = ti * T, T
            bl = blocks(n)
            FB = c.FC * T * 2
            V = stage_views([
                ("fT", 0, FB, BF16), ("u2", FB, DC * (T + 2) * 2, BF16), ("hal", FB + DC * (T + 2) * 2, DC * 2 * 2 * 2, BF16),
            ])
            fT = V["fT"].ap.rearrange("p (c t) -> p c t", c=c.FC)
            u2 = V["u2"].ap.rearrange("p (c t) -> p c t", c=DC)
            hal = V["hal"].ap.rearrange("p (s c t) -> p s c t", s=2, t=2)
            kb.dma("sp", u2[:, :, 0:T], u2scr[ti], reads=[o_u2], writes=[V["u2"].o], semobj=V["u2"].o)
            un, kk = ti // c.TPU, ti % c.TPU
            if kk == 0:
                kb.dma("sp", hal[:, 0], u2halo[:, :, 2 * un:2 * un + 2], reads=[o_edge], writes=[V["hal"].o], semobj=V["hal"].o)
                lcol = 0
            else:
                kb.dma("sp", hal[:, 0], u2edge[ti - 1], reads=[o_edge], writes=[V["hal"].o], semobj=V["hal"].o)
                lcol = 1
            if kk == c.TPU - 1:
                kb.dma("sp", hal[:, 1], u2halo[:, :, 2 * un:2 * un + 2], reads=[o_edge], writes=[V["hal"].o], semobj=V["hal"].o)
                rcol = 1
            else:
                kb.dma("sp", hal[:, 1], u2edge[ti + 1], reads=[o_edge], writes=[V["hal"].o], semobj=V["hal"].o)
                rcol = 0
            kb.op("dve", lambda e: e.tensor_copy(u2[:, :, T:T + 1], hal[:, 0, :, lcol:lcol + 1]), reads=[V["hal"].o], writes=[V["u2"].o])
            kb.op("dve", lambda e: e.tensor_copy(u2[:, :, T + 1:T + 2], hal[:, 1, :, rcol:rcol + 1]), reads=[V["hal"].o], writes=[V["u2"].o])
            if FFNSTOP <= 1:
                return
            rq = []
            for j in range(c.FC):
                rq += reqs_for("upa", [j]) + reqs_for("upg", [j])
            ws = WStream(rq)
            for j in range(c.FC if FFNSTOP > 1.5 else 2):
                ba, bao = pbank()
                bh, bho = pbank()
                wt, wo_ = ws.get()
                for kc in range(DC):
                    kb.op("pe", lambda e: e.matmul(ba[:, 0:n], wt[:, kc * 128:(kc + 1) * 128], u2[:, kc, 0:n], start=(kc == 0), stop=(kc == DC - 1)),
                          reads=[wo_, V["u2"].o], writes=[bao], sig=False)
                    kb.op("pe", lambda e: e.matmul(bh[:, 0:2], wt[:, kc * 128:(kc + 1) * 128], u2[:, kc, T:T + 2], start=(kc == 0), stop=(kc == DC - 1)),
                          reads=[wo_, V["u2"].o], writes=[bao, bho], sig=(kc == DC - 1))
                bg, bgo = fm_group(ws, "upg", j, u2, V["u2"].o, DC, n)
                a_sb, oa_sb = tf()
                kb.op("act", lambda e: e.activation(a_sb[:, 0:n], ba[:, 0:n], AF.Copy), reads=[bao], writes=[oa_sb])
                cv, ocv = tf()
                w0 = gc[:, G_CW + 3 * j:G_CW + 3 * j + 1]; w1 = gc[:, G_CW + 3 * j + 1:G_CW + 3 * j + 2]; w2 = gc[:, G_CW + 3 * j + 2:G_CW + 3 * j + 3]
                bcol = gc[:, G_CB + j:G_CB + j + 1]
                kb.op("dve", lambda e: e.tensor_scalar(cv[:, 0:n], ba[:, 0:n], w1, bcol, ALU.mult, ALU.add), reads=[bao, o_const], writes=[ocv])
                kb.op("dve", lambda e: e.scalar_tensor_tensor(cv[:, 1:n], a_sb[:, 0:n - 1], w0, cv[:, 1:n], ALU.mult, ALU.add), reads=[oa_sb, ocv, o_const], writes=[ocv])
                kb.op("dve", lambda e: e.scalar_tensor_tensor(cv[:, 0:n - 1], a_sb[:, 1:n], w2, cv[:, 0:n - 1], ALU.mult, ALU.add), reads=[oa_sb, ocv, o_const], writes=[ocv])
                kb.op("dve", lambda e: e.scalar_tensor_tensor(cv[:, 0:1], bh[:, 0:1], w0, cv[:, 0:1], ALU.mult, ALU.add), reads=[bho, ocv, o_const], writes=[ocv])
                kb.op("dve", lambda e: e.scalar_tensor_tensor(cv[:, n - 1:n], bh[:, 1:2], w2, cv[:, n - 1:n], ALU.mult, ALU.add), reads=[bho, ocv, o_const], writes=[ocv])
                ge, oge = tf()
                kb.op("act", lambda e: e.activation(ge[:, 0:n], cv[:, 0:n], AF.Gelu_apprx_tanh), reads=[ocv], writes=[oge])
                kb.op("dve", lambda e: e.tensor_tensor(fT[:, j, 0:n], bg[:, 0:n], ge[:, 0:n], ALU.mult), reads=[bgo, oge], writes=[V["fT"].o])
            if FFNSTOP <= 2:
                return
            V2 = stage_views([("fT", 0, FB, BF16), ("dT", FB, DC * T * 2, BF16)], carry={"fT": V["fT"].o})
            dT = V2["dT"].ap.rearrange("p (c t) -> p c t", c=DC)
            ws = WStream(reqs_for("dn", range(DC)))
            for j in range(DC):
                bk, bo = fm_group(ws, "dn", j, fT, V2["fT"].o, c.FC, n)
                kb.op("act" if j % 2 else "dve", (lambda e: e.activation(dT[:, j, 0:n], bk[:, 0:n], AF.Copy)) if j % 2 else
                      (lambda e: e.tensor_copy(dT[:, j, 0:n], bk[:, 0:n])), reads=[bo], writes=[V2["dT"].o])
            if FFNSTOP <= 3:
                return
            E0 = 0 if FB >= 3 * D * 4 + D * 2 else FB + DC * T * 2
            V6 = stage_views([("dT", FB, DC * T * 2, BF16), ("drow", E0, D * 4, F32), ("xrow", E0 + D * 4, D * 4, F32), ("junk", E0 + 2 * D * 4, D * 2, BF16),
                              ("grow", E0 + 2 * D * 4 + D * 2, D * 4, F32)],
                             carry={"dT": V2["dT"].o})
            emit_epilogue(V6, dT, n, t0, ti, bl, first=False)

        if PHASES >= 4:
            for i in range(c.NT):
                ffn_tile(i)
        for o_ in [o_kv, o_h, o_u2, o_edge] + [p_[3] for p_ in pieces]:
            kb._wait("pool", o_.w)
        kb._wait("pool", o_y.w)
        kb._wait("sp", o_y.w)
        global LAST_STATS
        LAST_STATS = dict(cnt=dict(kb.cnt), max_dma=max(kb.issued.values()), nsem=len(kb.sems))
    return nc


def make_inputs(c, x_seq, meta_tokens, P):
    perm = _axial_perm()
    NG = 2 * c.DC + c.QC + c.KC + 2 + 4 * c.FC
    g = np.zeros((128, NG), np.float32)
    col = lambda v: np.asarray(v, np.float32).reshape(-1, 128).T
    g[:, 0:c.DC] = col(P["g_pre_mix"]); g[:, c.DC:2 * c.DC] = col(P["g_pre_ffn"])
    o = 2 * c.DC
    g[:, o:o + c.QC] = col(P["g_cq"]); o += c.QC
    g[:, o:o + c.KC] = col(P["g_ckv"]); o += c.KC
    g[:, o] = np.asarray(P["g_qn"], np.float32).reshape(128)[perm]; o += 1
    g[:, o] = np.asarray(P["g_kn"], np.float32).reshape(128)[perm]; o += 1
    wc = np.asarray(P["w_conv"], np.float32).reshape(3, c.FC, 128)
    g[:, o:o + 3 * c.FC] = wc.transpose(2, 1, 0).reshape(128, 3 * c.FC); o += 3 * c.FC
    g[:, o:o + c.FC] = col(P["b_conv"])
    grows = np.stack([np.asarray(P["g_post_mix"], np.float32).reshape(-1), np.asarray(P["g_post_ffn"], np.float32).reshape(-1)])
    return g, grows


_PROG = {}


def core_plan(c):
    plan = []
    for core in range(8):
        g, c4 = divmod(core, 4)
        s0, s1, s2 = 3 * g, 3 * g + 1, 3 * g + 2
        if c4 == 0:
            A, B, units = s0, s0, [(s0, 0), (s0, 1), (s0, 2)]
        elif c4 == 1:
            A, B, units = s0, s1, [(s0, 3), (s1, 0), (s1, 1)]
        elif c4 == 2:
            A, B, units = s2, s1, [(s2, 0), (s1, 2), (s1, 3)]
        else:
            A, B, units = s2, s2, [(s2, 1), (s2, 2), (s2, 3)]
        plan.append((A, B, units))
    return plan


def run(c, x_prompt, x_sample, meta_tokens, P, wts):
    key = (c.D, c.SEQ, c.T, c.DFF)
    if key not in _PROG:
        _PROG[key] = build_program(c)
    nc = _PROG[key]
    seqs = [np.asarray(x_prompt[i], np.float32) for i in range(x_prompt.shape[0])] + [np.asarray(x_sample[i], np.float32) for i in range(x_sample.shape[0])]
    assert len(seqs) == 6
    wall = build_wall(c, *wts)
    ta, tb = rope_tables(c, *kv_positions(c), c.LP)
    g, grows = make_inputs(c, None, meta_tokens, P)
    ident = np.eye(128, dtype=np.float32)
    meta = np.asarray(meta_tokens, np.float32)
    plan = core_plan(c)
    U = c.UNIT
    in_maps = []
    for core in range(8):
        A, B, units = plan[core]
        x_kv = np.stack([np.concatenate([seqs[A], meta], axis=0), np.concatenate([seqs[B], meta], axis=0)])
        rows, pos, rid, cid = [], [], [], []
        hrows, hpos, hrid, hcid, hm = [], [], [], [], []
        for (sq, ch) in units:
            t = np.arange(ch * U, (ch + 1) * U)
            rows.append(seqs[sq][t]); pos.append(t + N_META); rid.append(t // GRID_W); cid.append(t % GRID_W)
            if ch == 0:
                hrows.append(meta[N_META - 1]); hpos.append(N_META - 1); hrid.append(0); hcid.append(0); hm.append(1.0)
            else:
                tl = ch * U - 1
                hrows.append(seqs[sq][tl]); hpos.append(tl + N_META); hrid.append(tl // GRID_W); hcid.append(tl % GRID_W); hm.append(1.0)
            if ch == c.SEQ // U - 1:
                hrows.append(seqs[sq][0]); hpos.append(N_META); hrid.append(0); hcid.append(0); hm.append(0.0)
            else:
                tr = (ch + 1) * U
                hrows.append(seqs[sq][tr]); hpos.append(tr + N_META); hrid.append(tr // GRID_W); hcid.append(tr % GRID_W); hm.append(1.0)
        x_q = np.ascontiguousarray(np.concatenate(rows + [np.stack(hrows)], axis=0))
        qpos = np.concatenate(pos + [np.array(hpos)]).astype(np.float32)
        qrid = np.concatenate(rid + [np.array(hrid)]).astype(np.float32)
        qcid = np.concatenate(cid + [np.array(hcid)]).astype(np.float32)
        tqa, tqb = rope_tables(c, qpos, qrid, qcid, c.NQP)
        hmask = np.zeros((1, 8), np.float32); hmask[0, :len(hm)] = hm
        in_maps.append({"x_kv": x_kv, "x_q": x_q, "tqA": tqa, "tqB": tqb, "hmask": hmask, "wall": wall, "tabA": ta, "tabB": tb,
                        "gcols": g, "grows": grows, "ident": ident})
    res = run_bass_kernel_spmd(nc, in_maps, core_ids=list(range(8)))
    outs = [np.empty((c.SEQ, c.D), np.float32) for _ in range(6)]
    for core in range(8):
        yc = res.results[core]["y"]
        for ui, (sq, ch) in enumerate(plan[core][2]):
            outs[sq][ch * U:(ch + 1) * U] = yc[ui * U:(ui + 1) * U]
    nb = x_prompt.shape[0]
    return np.stack(outs[:nb]).astype(np.float32), np.stack(outs[nb:]).astype(np.float32)


def kernel(x_prompt, x_sample, meta_tokens, g_pre_mix, w_in, g_cq, w_uq, g_ckv, w_ukv, g_qn, g_kn,
           w_pa, w_pb, w_o, g_post_mix, g_pre_ffn, w_up, w_conv, b_conv, w_down, g_post_ffn):
    c = FULL
    A = lambda a: np.asarray(a, np.float32)
    P = dict(g_pre_mix=A(g_pre_mix)[0], g_cq=A(g_cq)[0], g_ckv=A(g_ckv)[0], g_qn=A(g_qn)[0], g_kn=A(g_kn)[0],
             g_post_mix=A(g_post_mix)[0], g_pre_ffn=A(g_pre_ffn)[0], w_conv=A(w_conv)[0], b_conv=A(b_conv)[0], g_post_ffn=A(g_post_ffn)[0])
    wts = (A(w_in)[0], A(w_uq)[0], A(w_ukv)[0], A(w_pa)[0], A(w_pb)[0], A(w_o)[0], A(w_up)[0], A(w_down)[0])
    return run(c, A(x_prompt), A(x_sample), A(meta_tokens), P, wts)
```

```python
import contextlib
import math
import numpy as np
import concourse.bass as bass
import concourse.mybir as mybir
from concourse.bass_utils import run_bass_kernel_spmd

F32 = mybir.dt.float32
BF16 = mybir.dt.bfloat16
AF = mybir.ActivationFunctionType
ALU = mybir.AluOpType
EPS = 1e-6
N_META = 16
GRID_W = 64
ROPE_THETA = 10000.0


class Cfg:
    def __init__(self, D=4096, QL=1024, KVL=512, H=16, HQ=16, HKV=4, DFF=11008, SEQ=4096, T=512):
        self.D, self.QL, self.KVL, self.H, self.HQ, self.HKV, self.DFF, self.SEQ, self.T = D, QL, KVL, H, HQ, HKV, DFF, SEQ, T
        self.G = HQ // HKV
        self.DC, self.QC, self.KC, self.FC = D // 128, QL // 128, KVL // 128, DFF // 128
        self.L = SEQ + N_META
        self.NCH = SEQ // 128 + 1
        self.LP = self.NCH * 128
        self.NU = 3
        self.UNIT = SEQ // 4
        self.TPU = self.UNIT // T
        self.NT = self.NU * self.TPU
        self.NQ = self.NU * self.UNIT
        self.NH = 2 * self.NU
        self.NQP = self.NQ + 128
        self.in_splits = (QL, KVL, 64, HQ * 128, HKV * 128, HKV * 128, D, D)
        self.DIN = sum(self.in_splits)


FULL = Cfg()


def _kp(wcols, M):
    K, n = wcols.shape
    kc = K // 128
    nt = n // M
    a = wcols.reshape(kc, 128, nt, M).transpose(1, 2, 0, 3)
    return np.ascontiguousarray(a).reshape(128, nt * kc * M)


def _axial_perm():
    return np.concatenate([np.arange(0, 32), np.arange(64, 96), np.arange(32, 64), np.arange(96, 128)])


def blob_specs(c):
    return [
        ("ckv", c.KC, c.DC, 128), ("kr", 1, c.DC, 64), ("gk", c.HKV, c.DC, 128), ("gv", c.HKV, c.DC, 128),
        ("uk", c.H, c.KC, 128), ("uv", c.H, c.KC, 128),
        ("cq", c.QC, c.DC, 128), ("gq", c.HQ, c.DC, 128), ("uqn", c.H, c.QC, 128), ("uqr", c.H, c.QC, 64),
        ("ga", c.DC, c.DC, 128), ("pa", c.DC, c.H, 128), ("gb", c.DC, c.DC, 128), ("pb", c.DC, c.HQ, 128),
        ("wo", c.DC, c.DC, 128), ("upa", c.FC, c.DC, 128), ("upg", c.FC, c.DC, 128), ("dn", c.DC, c.FC, 128),
    ]


def blob_offsets(c):
    off, o = {}, 0
    for name, nt, kc, m in blob_specs(c):
        off[name] = (o, nt, kc, m)
        o += nt * kc * m
    return off, o


def build_wall(c, w_in, w_uq, w_ukv, w_pa, w_pb, w_o, w_up, w_down):
    offs = np.cumsum((0,) + c.in_splits)
    seg = lambda i: w_in[:, offs[i]:offs[i + 1]]
    perm = _axial_perm()
    hp = lambda w, nh: w.reshape(w.shape[0], nh, 128)[:, :, perm].reshape(w.shape[0], nh * 128)
    uq = w_uq.reshape(c.QL, c.H, 192)
    ukv = w_ukv.reshape(c.KVL, c.H, 256)
    parts = {
        "ckv": _kp(seg(1), 128), "kr": _kp(seg(2), 64), "gk": _kp(hp(seg(4), c.HKV), 128), "gv": _kp(seg(5), 128),
        "uk": _kp(ukv[:, :, :128].reshape(c.KVL, -1), 128), "uv": _kp(ukv[:, :, 128:].reshape(c.KVL, -1), 128),
        "cq": _kp(seg(0), 128), "gq": _kp(hp(seg(3), c.HQ), 128),
        "uqn": _kp(uq[:, :, :128].reshape(c.QL, -1), 128), "uqr": _kp(uq[:, :, 128:].reshape(c.QL, -1), 64),
        "ga": _kp(seg(6), 128), "pa": _kp(w_pa, 128), "gb": _kp(seg(7), 128), "pb": _kp(w_pb, 128),
        "wo": _kp(w_o, 128), "upa": _kp(w_up[:, :c.DFF], 128), "upg": _kp(w_up[:, c.DFF:], 128), "dn": _kp(w_down, 128),
    }
    return np.concatenate([parts[n] for n, *_ in blob_specs(c)], axis=1)


def kv_positions(c):
    pos = np.concatenate([np.arange(N_META, c.L), np.arange(N_META)]).astype(np.float32)
    rows = c.SEQ // GRID_W
    row_ids = np.concatenate([np.repeat(np.arange(rows), GRID_W), np.zeros(N_META)]).astype(np.float32)
    col_ids = np.concatenate([np.tile(np.arange(GRID_W), rows), np.zeros(N_META)]).astype(np.float32)
    return pos, row_ids, col_ids


def rope_tables(c, pos, row_ids, col_ids, width):
    def tab(p, dim):
        inv = (ROPE_THETA ** (-np.arange(0, dim, 2, dtype=np.float32) / np.float32(dim))).astype(np.float32)
        ang = p[None, :] * inv[:, None]
        return np.cos(ang).astype(np.float32), np.sin(ang).astype(np.float32)

    c1, s1 = tab(pos, 64)
    rc, rs = tab(row_ids, 64)
    cc, cs = tab(col_ids, 64)
    n_ = pos.shape[0]
    ta = np.zeros((128, 2, width), np.float32)
    ta[:64, 0, :n_] = np.concatenate([c1, c1]); ta[:64, 1, :n_] = np.concatenate([s1, -s1])
    tb = np.zeros((128, 2, width), np.float32)
    tb[:, 0, :n_] = np.concatenate([rc, cc, rc, cc]); tb[:, 1, :n_] = np.concatenate([rs, cs, -rs, -cs])
    return ta, tb


class Obj:
    __slots__ = ("name", "w", "r", "dsem", "dcnt", "psum")

    def __init__(self, name, psum=False):
        self.name, self.w, self.r, self.dsem, self.dcnt, self.psum = name, {}, {}, None, 0, psum


class KB:
    def __init__(self, nc, es):
        self.nc, self.es = nc, es
        self.eng = dict(pe=nc.tensor, act=nc.scalar, dve=nc.vector, pool=nc.gpsimd, sp=nc.sync)
        self.sems = []
        self.tl = {e: self.newsem("tl_" + e) for e in ("pe", "act", "dve", "pool")}
        self.cnt = {e: 0 for e in self.tl}
        self.waited = {e: {} for e in self.eng}
        self.pe_pending = []
        self.issued = {}
        self.named = {}

    def newsem(self, name):
        s = self.es.enter_context(self.nc.semaphore(name))
        self.sems.append(s)
        return len(self.sems) - 1

    def _wait(self, e, evs):
        for si, val in evs.items():
            if e == "pe" and si == self.tl["pe"]:
                continue
            if si in self.issued:
                val = self.issued[si]
            if self.waited[e].get(si, 0) >= val:
                continue
            self.eng[e].wait_ge(self.sems[si], val)
            self.waited[e][si] = val

    def op(self, e, fn, reads=(), writes=(), sig=True):
        for o in reads:
            self._wait(e, o.w)
            if o.psum:
                self._wait(e, {k: v for k, v in o.r.items() if k != self.tl.get(e)})
        for o in writes:
            self._wait(e, o.w)
            self._wait(e, o.r)
        ins = fn(self.eng[e])
        if sig:
            self.cnt[e] += 1
            ins.then_inc(self.sems[self.tl[e]], 1)
            si, v = self.tl[e], self.cnt[e]
            for o in reads:
                o.r[si] = v
            for o in writes:
                o.w[si] = v
            if e == "pe":
                for o in self.pe_pending:
                    o.r[si] = v
                self.pe_pending = []
        else:
            assert e == "pe"
            self.pe_pending.extend(reads)
        return ins

    def dma(self, q, out_ap, in_ap, reads=(), writes=(), semobj=None):
        for o in reads:
            self._wait(q, o.w)
        for o in writes:
            self._wait(q, o.w)
            self._wait(q, o.r)
        so = semobj
        if so.dsem is None:
            if so.name not in self.named:
                self.named[so.name] = self.newsem("d_" + so.name)
                self.issued[self.named[so.name]] = 0
            so.dsem = self.named[so.name]
        self.issued[so.dsem] += 16
        self.eng[q].dma_start(out=out_ap, in_=in_ap).then_inc(self.sems[so.dsem], 16)
        for o in reads:
            o.r[so.dsem] = self.issued[so.dsem]
        for o in writes:
            o.w[so.dsem] = self.issued[so.dsem]

    def inherit(self, new_objs, old_objs):
        u = {}
        for o in old_objs:
            for d in (o.w, o.r):
                for k, v in d.items():
                    u[k] = max(u.get(k, 0), v)
        for o in new_objs:
            for k, v in u.items():
                o.w[k] = max(o.w.get(k, 0), v)
                o.r[k] = max(o.r.get(k, 0), v)


DEBUG = False
LAST_STATS = None
PHASES = 4
KVSTOP = 99
FFNSTOP = 99
EPSTOP = 99


def build_program(c):
    nc = bass.Bass("TRN2", target_bir_lowering=False)
    T, D, DC, L, LP, NCH = c.T, c.D, c.DC, c.L, c.LP, c.NCH
    woff, XTOT = blob_offsets(c)
    dt_in = lambda n, s: nc.dram_tensor(n, s, F32, kind="ExternalInput").ap()
    x_kv = dt_in("x_kv", [2, L, D])
    x_q = dt_in("x_q", [c.NQ + c.NH, D])
    tqA = dt_in("tqA", [128, 2, c.NQP])
    tqB = dt_in("tqB", [128, 2, c.NQP])
    hmask_in = dt_in("hmask", [1, 8])
    wall = dt_in("wall", [128, XTOT])
    tabA = dt_in("tabA", [128, 2, LP])
    tabB = dt_in("tabB", [128, 2, LP])
    gcols = dt_in("gcols", [128, 2 * DC + c.QC + c.KC + 2 + 4 * c.FC])
    grows = dt_in("grows", [2, D])
    ident_in = dt_in("ident", [128, 128])
    y = nc.dram_tensor("y", [c.NQ, D], F32, kind="ExternalOutput").ap()
    wbf = {name: nc.dram_tensor("wbf_" + name, [128, nt_ * kc_ * m_], BF16).ap() for name, nt_, kc_, m_ in blob_specs(c)}
    kTa_ = [nc.dram_tensor("kTa%d" % i, [c.H, 128, LP], BF16).ap() for i in range(2)]
    krT_ = [nc.dram_tensor("krT%d" % i, [64, LP], BF16).ap() for i in range(2)]
    va_ = [nc.dram_tensor("va%d" % i, [LP, c.H, 128], BF16).ap() for i in range(2)]
    kTb_ = [nc.dram_tensor("kTb%d" % i, [c.HKV, 128, LP], BF16).ap() for i in range(2)]
    vb_ = [nc.dram_tensor("vb%d" % i, [LP, c.HKV, 128], BF16).ap() for i in range(2)]
    hscr = nc.dram_tensor("hscr", [c.NQ, D], F32).ap()
    u2scr = nc.dram_tensor("u2scr", [c.NT, 128, DC, T], BF16).ap()
    u2edge = nc.dram_tensor("u2edge", [c.NT, 128, DC, 2], BF16).ap()
    u2halo = nc.dram_tensor("u2halo", [128, DC, 8], BF16).ap()

    es = contextlib.ExitStack()
    with es:
        kb = KB(nc, es)
        sbt = lambda n, s, d: es.enter_context(nc.sbuf_tensor(n, s, d))
        ident_f = sbt("ident_f", [128, 128], F32)
        ident_b = sbt("ident_b", [128, 128], BF16)
        ones_f = sbt("ones_f", [128, 128], F32)
        ones_q = sbt("ones_q", [128, 128], BF16)
        ones_k = sbt("ones_k", [128, 128], BF16)
        ones_h = sbt("ones_h", [128, 128], BF16)
        NG = 2 * DC + c.QC + c.KC + 2 + 4 * c.FC
        gc = sbt("gc", [128, NG], F32)
        zed = sbt("zed", [128, DC * 2], BF16)
        o_const = Obj("const")
        G_PRE, G_FFN = 0, DC
        G_CQ = 2 * DC; G_CKV = G_CQ + c.QC; G_QN = G_CKV + c.KC; G_KN = G_QN + 1
        G_CW = G_KN + 1; G_CB = G_CW + 3 * c.FC
        kb.dma("sp", ident_f[:], ident_in[:], writes=[o_const], semobj=o_const)
        kb.dma("sp", gc[:], gcols[:], writes=[o_const], semobj=o_const)
        kb.op("dve", lambda e: e.tensor_copy(ident_b[:], ident_f[:]), reads=[o_const], writes=[o_const])
        kb.op("dve", lambda e: e.memset(ones_f[:], 1.0), writes=[o_const])
        kb.op("dve", lambda e: e.memset(ones_q[:], 1.0 / c.QL), writes=[o_const])
        kb.op("dve", lambda e: e.memset(ones_k[:], 1.0 / c.KVL), writes=[o_const])
        kb.op("dve", lambda e: e.memset(ones_h[:], 1.0 / 128), writes=[o_const])
        kb.op("dve", lambda e: e.memset(zed[:], 0.0), writes=[o_const])
        o_edge = Obj("u2edge")
        hmask = sbt("hmask_sb", [128, 8], F32)
        kb.dma("sp", hmask[:], hmask_in[0:1, :].broadcast_to([128, 8]), writes=[o_const], semobj=o_const)

        pieces = []
        PW = 65536
        for name, nt_, kc_, m_ in blob_specs(c):
            g0 = woff[name][0]
            tot = nt_ * kc_ * m_
            for a in range(0, tot, PW):
                b = min(tot, a + PW)
                po = Obj("wp_%s_%d" % (name, a))
                so = Obj("wcast_" + name)
                kb.dma("pool", wbf[name][:, a:b], wall[:, g0 + a:g0 + b], writes=[po], semobj=so)
                pieces.append((name, a, b, po))

        def wobjs(name, a, b):
            return [po for (pn, pa, pb, po) in pieces if pn == name and pa < b and a < pb]

        NSLOT = 3
        SLOT = 4096
        wslots = [sbt("wslot%d" % i, [128, SLOT], BF16) for i in range(NSLOT)]
        wso = [Obj("wslot%d" % i) for i in range(NSLOT)]
        wstate = {"n": 0}

        class WStream:
            def __init__(self, reqs):
                self.reqs, self.issued, self.got = reqs, 0, 0
                self.slot = {}

            def _issue(self):
                name, off, n = self.reqs[self.issued]
                s = wstate["n"] % NSLOT
                wstate["n"] += 1
                kb.dma("sp", wslots[s][:, 0:n], wbf[name][:, off:off + n], reads=wobjs(name, off, off + n), writes=[wso[s]], semobj=wso[s])
                self.slot[self.issued] = s
                self.issued += 1

            def get(self):
                while self.issued < len(self.reqs) and self.issued < self.got + NSLOT - 1:
                    self._issue()
                if self.issued <= self.got:
                    self._issue()
                s = self.slot[self.got]
                self.got += 1
                return wslots[s], wso[s]

        def tile_req(name, j, kc0=0, kc1=None):
            o, nt, kc, m = woff[name]
            kc1 = kc if kc1 is None else kc1
            return (name, (j * kc + kc0) * m, (kc1 - kc0) * m)

        banks = [es.enter_context(nc.psum_tensor("bank%d" % i, [128, 512], F32)) for i in range(8)]
        bobj = [Obj("bank%d" % i, psum=True) for i in range(8)]
        pst = {"n": 0}

        def pbank():
            i = pst["n"] % 6
            pst["n"] += 1
            return banks[i], bobj[i]

        ARENA = 148 * 1024
        arena = sbt("arena", [128, ARENA // 2], BF16)
        cur_objs = []

        class View:
            def __init__(self, off, nbytes, dt, name):
                assert off % 4 == 0 and off + nbytes <= ARENA, (name, off, nbytes)
                a = arena[:, off // 2:(off + nbytes) // 2]
                self.ap = a.bitcast(F32) if dt == F32 else a
                self.o = Obj(name)
                self.end = off + nbytes

        def stage_views(specs, carry=None):
            nonlocal cur_objs
            carry = carry or {}
            vs = {n: View(off, nb, dt, n) for (n, off, nb, dt) in specs}
            for n_, o_ in carry.items():
                vs[n_].o = o_
            kb.inherit([v.o for n_, v in vs.items() if n_ not in carry], cur_objs)
            cur_objs = [v.o for v in vs.values()]
            return vs

        rsA = sbt("rsA", [128, T], F32); rsB = sbt("rsB", [128, T], F32)
        o_rsA, o_rsB = Obj("rsA"), Obj("rsB")
        st1 = sbt("st1", [128, 8], F32); o_st1 = Obj("st1")
        tA = sbt("tA", [128, 2, T], F32); tB = sbt("tB", [128, 2, T], F32)
        o_tA, o_tB = Obj("tA"), Obj("tB")
        NTMP = 4
        tmpf = [sbt("tmpf%d" % i, [128, T], F32) for i in range(NTMP)]
        o_tmpf = [Obj("tmpf%d" % i) for i in range(NTMP)]
        tmpb = [sbt("tmpb%d" % i, [128, T], BF16) for i in range(4)]
        o_tmpb = [Obj("tmpb%d" % i) for i in range(4)]
        rr = {"f": 0, "b": 0}

        def tf():
            i = rr["f"] % NTMP; rr["f"] += 1
            return tmpf[i], o_tmpf[i]

        def tb_():
            i = rr["b"] % 4; rr["b"] += 1
            return tmpb[i], o_tmpb[i]

        def blocks(n):
            return [(i, min(128, n - i)) for i in range(0, n, 128)]

        def emit_xnormT(xrows, n, gcol0, xt, o_xt, xn, o_xn, uT, o_uT, junk, o_junk):
            bl = blocks(n)
            for bi, (t0, nt) in enumerate(bl):
                kb.dma("sp", xt[0:nt, bi % 2, :], xrows[t0:t0 + nt, :], writes=[o_xt[bi % 2]], semobj=o_xt[bi % 2])
                kb.op("act", lambda e: e.activation(junk[0:nt, :], xt[0:nt, bi % 2, :], AF.Square, accum_out=st1[0:nt, 0:1]),
                      reads=[o_xt[bi % 2]], writes=[o_junk, o_st1])
                kb.op("act", lambda e: e.activation(st1[0:nt, 1:2], st1[0:nt, 0:1], AF.Sqrt, bias=EPS, scale=1.0 / D),
                      reads=[o_st1], writes=[o_st1])
                kb.op("dve", lambda e: e.reciprocal(st1[0:nt, 2:3], st1[0:nt, 1:2]), reads=[o_st1], writes=[o_st1])
                kb.op("dve", lambda e: e.tensor_scalar(xn[0:nt, bi, :], xt[0:nt, bi % 2, :], st1[0:nt, 2:3], None, ALU.mult),
                      reads=[o_xt[bi % 2], o_st1], writes=[o_xn])
            for cch in range(DC):
                bk, bo = pbank()
                bkb = bk.bitcast(BF16)
                for bi, (t0, nt) in enumerate(bl):
                    kb.op("pe", lambda e: e.transpose(bkb[:, t0:t0 + nt], xn[0:nt, bi, cch * 128:(cch + 1) * 128], ident_b[0:nt, 0:nt]),
                          reads=[o_xn, o_const], writes=[bo], sig=(bi == len(bl) - 1))
                eng = "dve" if cch % 2 == 0 else "act"
                if eng == "dve":
                    kb.op("dve", lambda e: e.tensor_scalar(uT[:, cch, 0:n], bkb[:, 0:n], gc[:, gcol0 + cch:gcol0 + cch + 1], None, ALU.mult),
                          reads=[bo, o_const], writes=[o_uT])
                else:
                    kb.op("act", lambda e: e.activation(uT[:, cch, 0:n], bkb[:, 0:n], AF.Copy, scale=gc[:, gcol0 + cch:gcol0 + cch + 1]),
                          reads=[bo, o_const], writes=[o_uT])

        def fm_group(ws, name, j, act, o_act, kcn, n, M=128, bank=None, kcsplit=32):
            bk, bo = bank if bank is not None else pbank()
            for k0 in range(0, kcn, kcsplit):
                k1 = min(kcn, k0 + kcsplit)
                wt, wo_ = ws.get()
                for kc in range(k0, k1):
                    last = kc == k1 - 1
                    kb.op("pe", lambda e: e.matmul(bk[0:M, 0:n], wt[:, (kc - k0) * M:(kc - k0 + 1) * M], act[:, kc, 0:n],
                                                   start=(kc == 0), stop=(kc == kcn - 1)),
                          reads=[wo_, o_act], writes=[bo], sig=last)
            return bk, bo

        def reqs_for(name, js, kcsplit=32):
            o, nt, kc, m = woff[name]
            r = []
            for j in js:
                for k0 in range(0, kc, kcsplit):
                    r.append(tile_req(name, j, k0, min(kc, k0 + kcsplit)))
            return r

        def rstd_from_mean(bk, bo, M, n, dst, o_dst):
            kb.op("act", lambda e: e.activation(dst[0:M, 0:n], bk[0:M, 0:n], AF.Sqrt, bias=EPS, scale=1.0), reads=[bo], writes=[o_dst])
            kb.op("dve", lambda e: e.reciprocal(dst[0:M, 0:n], dst[0:M, 0:n]), reads=[o_dst], writes=[o_dst])

        def emit_rope(src, o_src, M, n, tab, o_tab, dst_ap, o_dst):
            hf = M // 2
            t1, o1 = tf()
            t2, o2 = tf()
            kb.op("dve", lambda e: e.tensor_tensor(t1[0:M, 0:n], src[0:M, 0:n], tab[0:M, 0, 0:n], ALU.mult), reads=[o_src, o_tab], writes=[o1])
            kb.op("dve", lambda e: e.tensor_tensor(t2[0:hf, 0:n], src[hf:M, 0:n], tab[hf:M, 1, 0:n], ALU.mult), reads=[o_src, o_tab], writes=[o2])
            kb.op("dve", lambda e: e.tensor_tensor(t2[hf:M, 0:n], src[0:hf, 0:n], tab[0:hf, 1, 0:n], ALU.mult), reads=[o_src, o_tab], writes=[o2])
            kb.op("dve", lambda e: e.tensor_tensor(dst_ap, t1[0:M, 0:n], t2[0:M, 0:n], ALU.add), reads=[o1, o2], writes=[o_dst])

        def emit_headnorm_rope(bk, bo, n, gcol, tab, o_tab, dst_ap, o_dst):
            sq, osq = tb_()
            kb.op("act", lambda e: e.activation(sq[:, 0:n], bk[:, 0:n], AF.Square), reads=[bo], writes=[osq])
            b2, bo2 = pbank()
            kb.op("pe", lambda e: e.matmul(b2[:, 0:n], ones_h[:], sq[:, 0:n], start=True, stop=True), reads=[osq, o_const], writes=[bo2])
            rs, ors = tf()
            rstd_from_mean(b2, bo2, 128, n, rs, ors)
            xg, oxg = tf()
            kb.op("dve", lambda e: e.scalar_tensor_tensor(xg[:, 0:n], bk[:, 0:n], gc[:, gcol:gcol + 1], rs[:, 0:n], ALU.mult, ALU.mult),
                  reads=[bo, ors, o_const], writes=[oxg])
            emit_rope(xg, oxg, 128, n, tab, o_tab, dst_ap, o_dst)

        def load_tabs(t0, n, q=False):
            kb.dma("sp", tA[:, :, 0:n], (tqA if q else tabA)[:, :, t0:t0 + n], writes=[o_tA], semobj=o_tA)
            kb.dma("sp", tB[:, :, 0:n], (tqB if q else tabB)[:, :, t0:t0 + n], writes=[o_tB], semobj=o_tB)

        o_kv = Obj("kvscratch")
        o_dbg = Obj("dbg")

        def dbg(name, view_ap, vobj, shape):
            if not DEBUG:
                return
            t = nc.dram_tensor("dbg_" + name, shape, BF16).ap()
            kb.dma("pool", t, view_ap, reads=[vobj], writes=[o_dbg], semobj=o_dbg)

        def kv_tile(sset, t0, n):
            kTa, krT, va, kTb, vb = kTa_[sset], krT_[sset], va_[sset], kTb_[sset], vb_[sset]
            V = stage_views([
                ("xt0", 0, D * 4, F32), ("xt1", D * 4, D * 4, F32), ("xn", 32768, 4 * D * 2, BF16), ("uT", 65536, DC * T * 2, BF16),
                ("junk", 98304, D * 2, BF16), ("zc", 106496, c.KC * T * 4, F32), ("ckvn", 106496 + c.KC * T * 4, c.KC * T * 2, BF16),
                ("kst0", 122880, 4 * T * 2, BF16), ("kst1", 122880 + 4 * T * 2, 4 * T * 2, BF16),
                ("vst0", 122880 + 8 * T * 2, 4096, BF16), ("vst1", 122880 + 8 * T * 2 + 4096, 4096, BF16),
                ("krs", 122880 + 8 * T * 2 + 8192, T * 2, BF16),
            ])
            xt = arena[:, 0:4 * D].bitcast(F32).rearrange("p (b d) -> p b d", b=2)
            o_xt = [V["xt0"].o, V["xt1"].o]
            xn = V["xn"].ap.rearrange("p (b d) -> p b d", b=4)
            uT = V["uT"].ap.rearrange("p (c t) -> p c t", c=DC)
            zc = V["zc"].ap.rearrange("p (c t) -> p c t", c=c.KC)
            ckvn = V["ckvn"].ap.rearrange("p (c t) -> p c t", c=c.KC)
            kstg = [V["kst%d" % i].ap.rearrange("p (h t) -> p h t", h=4) for i in range(2)]
            vstg = [V["vst%d" % i].ap.rearrange("p (b e) -> p b e", b=4) for i in range(2)]
            krs = V["krs"].ap
            grp = {"k": 0, "v": 0}
            load_tabs(t0, n)
            emit_xnormT(x_kv[sset, t0:t0 + n, :], n, G_PRE, xt, o_xt, xn, V["xn"].o, uT, V["uT"].o, V["junk"].ap, V["junk"].o)
            o_uT = V["uT"].o
            if KVSTOP <= 1:
                return
            bl = blocks(n)
            ws = WStream(reqs_for("ckv", range(c.KC)) + reqs_for("kr", [0]) + reqs_for("gk", range(c.HKV)) + reqs_for("gv", range(c.HKV))
                         + reqs_for("uk", range(c.H)) + reqs_for("uv", range(c.H)))
            sqs = []
            bS, boS = banks[7], bobj[7]
            for j in range(c.KC):
                bk, bo = fm_group(ws, "ckv", j, uT, o_uT, DC, n)
                if KVSTOP <= 1.2:
                    continue
                kb.op("dve", lambda e: e.tensor_copy(zc[:, j, 0:n], bk[:, 0:n]), reads=[bo], writes=[V["zc"].o])
                sq, osq = tb_()
                kb.op("act", lambda e: e.activation(sq[:, 0:n], bk[:, 0:n], AF.Square), reads=[bo], writes=[osq])
                if KVSTOP <= 1.5:
                    continue
                kb.op("pe", lambda e: e.matmul(bS[:, 0:n], ones_k[:], sq[:, 0:n], start=(j == 0), stop=(j == c.KC - 1)),
                      reads=[osq, o_const], writes=[boS], sig=True)
            if KVSTOP <= 1.7:
                return
            rstd_from_mean(bS, boS, 128, n, rsA, o_rsA)
            if KVSTOP <= 1.9:
                return
            for j in range(c.KC):
                kb.op("dve", lambda e: e.scalar_tensor_tensor(ckvn[:, j, 0:n], zc[:, j, 0:n], gc[:, G_CKV + j:G_CKV + j + 1], rsA[:, 0:n], ALU.mult, ALU.mult),
                      reads=[V["zc"].o, o_rsA, o_const], writes=[V["ckvn"].o])
            if KVSTOP <= 2:
                return
            bk, bo = fm_group(ws, "kr", 0, uT, o_uT, DC, n, M=64)
            xr, oxr = tf()
            kb.op("dve", lambda e: e.tensor_copy(xr[0:64, 0:n], bk[0:64, 0:n]), reads=[bo], writes=[oxr])
            emit_rope(xr, oxr, 64, n, tA, o_tA, krs[0:64, 0:n], V["krs"].o)
            kb.dma("pool", krT[:, t0:t0 + n], krs[0:64, 0:n], reads=[V["krs"].o], writes=[o_kv], semobj=V["krs"].o)
            if KVSTOP <= 3:
                return
            def kgroup_store(dst, h0, nh):
                gi = grp["k"] % 2
                grp["k"] += 1
                return gi, kstg[gi], V["kst%d" % gi].o

            def kgroup_flush(dst, h0, nh, gi):
                kb.dma("pool", dst[h0:h0 + nh, :, t0:t0 + n].rearrange("h p t -> p h t"), kstg[gi][:, 0:nh, 0:n],
                       reads=[V["kst%d" % gi].o], writes=[o_kv], semobj=V["kst%d" % gi].o)

            for h0 in range(0, c.HKV, 4):
                nh = min(4, c.HKV - h0)
                gi, kg, ko = kgroup_store(kTb, h0, nh)
                for hh in range(nh):
                    bk, bo = fm_group(ws, "gk", h0 + hh, uT, o_uT, DC, n)
                    emit_headnorm_rope(bk, bo, n, G_KN, tB, o_tB, kg[:, hh, 0:n], ko)
                kgroup_flush(kTb, h0, nh, gi)

            if KVSTOP <= 4:
                return

            def emit_v(ws_, name, h, act, o_act, kcn, dst, nheads):
                bk, bo = fm_group(ws_, name, h, act, o_act, kcn, n)
                vt, ovt = tb_()
                kb.op("act", lambda e: e.activation(vt[:, 0:n], bk[:, 0:n], AF.Copy), reads=[bo], writes=[ovt])
                b2, bo2 = pbank()
                b2b = b2.bitcast(BF16)
                for bi, (s0, nt) in enumerate(bl):
                    kb.op("pe", lambda e: e.transpose(b2b[0:nt, bi * 128:(bi + 1) * 128], vt[:, s0:s0 + nt], ident_b[:, :]),
                          reads=[ovt, o_const], writes=[bo2], sig=(bi == len(bl) - 1))
                nb = len(bl)
                hh = h % 4
                if hh == 0:
                    grp["v"] += 1
                gi = grp["v"] % 2
                vg, vo = vstg[gi], V["vst%d" % gi].o
                if n % 128 == 0:
                    kb.op("dve", lambda e: e.tensor_copy(vg[:, 0:nb, hh * 128:(hh + 1) * 128], b2b[:, 0:nb * 128].rearrange("p (b d) -> p b d", b=nb)),
                          reads=[bo2], writes=[vo])
                else:
                    kb.op("dve", lambda e: e.tensor_copy(vg[0:n, 0, hh * 128:(hh + 1) * 128], b2b[0:n, 0:128]), reads=[bo2], writes=[vo])
                if hh == 3 or h == nheads - 1:
                    h0 = h - hh
                    nh = hh + 1
                    if n % 128 == 0:
                        kb.dma("pool", dst[t0:t0 + n, h0:h0 + nh, :].rearrange("(b p) h d -> p b (h d)", p=128), vg[:, 0:nb, 0:nh * 128],
                               reads=[vo], writes=[o_kv], semobj=vo)
                    else:
                        kb.dma("pool", dst[t0:t0 + n, h0:h0 + nh, :].rearrange("t h d -> t (h d)"), vg[0:n, 0, 0:nh * 128],
                               reads=[vo], writes=[o_kv], semobj=vo)

            for h in range(c.HKV):
                emit_v(ws, "gv", h, uT, o_uT, DC, vb, c.HKV)
            if KVSTOP <= 5:
                return
            for h0 in range(0, c.H, 4):
                nh = min(4, c.H - h0)
                gi, kg, ko = kgroup_store(kTa, h0, nh)
                for hh in range(nh):
                    bk, bo = fm_group(ws, "uk", h0 + hh, ckvn, V["ckvn"].o, c.KC, n)
                    kb.op("act", lambda e: e.activation(kg[:, hh, 0:n], bk[:, 0:n], AF.Copy), reads=[bo], writes=[ko])
                kgroup_flush(kTa, h0, nh, gi)
            for h in range(c.H):
                emit_v(ws, "uv", h, ckvn, V["ckvn"].o, c.KC, va, c.H)

        if PHASES >= 2:
            for sset in range(2):
                for i in range(c.SEQ // T):
                    kv_tile(sset, i * T, T)
                kv_tile(sset, c.SEQ, N_META)

        o_h = Obj("hscr"); o_u2 = Obj("u2scr")
        HCH = (NCH + 1) // 2
        KVB = HCH * 128 * 2

        def attn_tile(t0, n, ti):
            bl = blocks(n)
            nb = len(bl)
            V = stage_views([
                ("xt0", 0, D * 4, F32), ("xt1", D * 4, D * 4, F32), ("xn", 32768, 4 * D * 2, BF16), ("uT", 65536, DC * T * 2, BF16),
                ("junk", 98304, D * 2, BF16),
                ("qan", 106496, 0, BF16),
            ][:5])
            xt = arena[:, 0:4 * D].bitcast(F32).rearrange("p (b d) -> p b d", b=2)
            o_xt = [V["xt0"].o, V["xt1"].o]
            xn = V["xn"].ap.rearrange("p (b d) -> p b d", b=4)
            uT = V["uT"].ap.rearrange("p (c t) -> p c t", c=DC)
            o_uT = V["uT"].o
            load_tabs(t0, n, q=True)
            emit_xnormT(x_q[t0:t0 + n, :], n, G_PRE, xt, o_xt, xn, V["xn"].o, uT, o_uT, V["junk"].ap, V["junk"].o)
            QB = 65536 + max(DC, c.H + c.HQ) * T * 2
            V2 = stage_views([
                ("uT", 65536, DC * T * 2, BF16),
                ("zq", 0, c.QC * T * 4, F32), ("cqn", c.QC * T * 4, c.QC * T * 2, BF16),
                ("qan", QB, c.H * T * 2, BF16), ("qar", QB + c.H * T * 2, c.H * T * 2, BF16), ("qb", QB + 2 * c.H * T * 2, c.HQ * T * 2, BF16),
            ], carry={"uT": o_uT})
            zq = V2["zq"].ap.rearrange("p (c t) -> p c t", c=c.QC)
            cqn = V2["cqn"].ap.rearrange("p (c t) -> p c t", c=c.QC)
            qan = V2["qan"].ap.rearrange("p (h t) -> p h t", h=c.H)
            qar = V2["qar"].ap.rearrange("p (h t) -> p h t", h=c.H)
            qb = V2["qb"].ap.rearrange("p (h t) -> p h t", h=c.HQ)
            ws = WStream(reqs_for("cq", range(c.QC)) + reqs_for("gq", range(c.HQ)) + reqs_for("uqn", range(c.H)) + reqs_for("uqr", range(c.H)))
            bS, boS = banks[7], bobj[7]
            for j in range(c.QC):
                bk, bo = fm_group(ws, "cq", j, uT, o_uT, DC, n)
                kb.op("dve", lambda e: e.tensor_copy(zq[:, j, 0:n], bk[:, 0:n]), reads=[bo], writes=[V2["zq"].o])
                sq, osq = tb_()
                kb.op("act", lambda e: e.activation(sq[:, 0:n], bk[:, 0:n], AF.Square), reads=[bo], writes=[osq])
                kb.op("pe", lambda e: e.matmul(bS[:, 0:n], ones_q[:], sq[:, 0:n], start=(j == 0), stop=(j == c.QC - 1)),
                      reads=[osq, o_const], writes=[boS], sig=True)
            rstd_from_mean(bS, boS, 128, n, rsA, o_rsA)
            for j in range(c.QC):
                kb.op("dve", lambda e: e.scalar_tensor_tensor(cqn[:, j, 0:n], zq[:, j, 0:n], gc[:, G_CQ + j:G_CQ + j + 1], rsA[:, 0:n], ALU.mult, ALU.mult),
                      reads=[V2["zq"].o, o_rsA, o_const], writes=[V2["cqn"].o])
            for h in range(c.HQ):
                bk, bo = fm_group(ws, "gq", h, uT, o_uT, DC, n)
                emit_headnorm_rope(bk, bo, n, G_QN, tB, o_tB, qb[:, h, 0:n], V2["qb"].o)
            for h in range(c.H):
                bk, bo = fm_group(ws, "uqn", h, cqn, V2["cqn"].o, c.QC, n)
                kb.op("act", lambda e: e.activation(qan[:, h, 0:n], bk[:, 0:n], AF.Copy), reads=[bo], writes=[V2["qan"].o])
            for h in range(c.H):
                bk, bo = fm_group(ws, "uqr", h, cqn, V2["cqn"].o, c.QC, n, M=64)
                xr, oxr = tf()
                kb.op("dve", lambda e: e.tensor_copy(xr[0:64, 0:n], bk[0:64, 0:n]), reads=[bo], writes=[oxr])
                emit_rope(xr, oxr, 64, n, tA, o_tA, qar[0:64, h, 0:n], V2["qar"].o)
            specs = [("qan", QB, c.H * T * 2, BF16), ("qar", QB + c.H * T * 2, c.H * T * 2, BF16), ("qb", QB + 2 * c.H * T * 2, c.HQ * T * 2, BF16),
                     ("oa", 65536, c.H * T * 2, BF16), ("ob", 65536 + c.H * T * 2, c.HQ * T * 2, BF16),
                     ("kr", 0, LP * 2, BF16)]
            base = LP * 2
            for i in range(2):
                specs.append(("kh%d" % i, base + i * 2 * KVB, KVB, BF16))
                specs.append(("vh%d" % i, base + i * 2 * KVB + KVB, KVB, BF16))
            PB = base + 4 * KVB
            for i in range(4):
                specs.append(("pt%d" % i, PB + i * T * 2, T * 2, BF16))
            AB = PB + 4 * T * 2
            for i in range(2):
                specs.append(("acc%de" % i, AB + i * 2 * T * 4, T * 4, F32)); specs.append(("acc%do" % i, AB + i * 2 * T * 4 + T * 4, T * 4, F32))
            specs.append(("rcp", AB + 4 * T * 4, T * 4, F32))
            assert AB + 5 * T * 4 <= 65536
            V3 = stage_views(specs, carry={k_: V2[k_].o for k_ in ("qan", "qar", "qb")})
            oa = V3["oa"].ap.rearrange("p (h t) -> p h t", h=c.H)
            ob = V3["ob"].ap.rearrange("p (h t) -> p h t", h=c.HQ)
            kr_sb = V3["kr"].ap
            hb = {"n": 0, "o": 0}
            if ti >= 0:
                passes = [(0 if ti < c.TPU else 1, 0, n)]
            else:
                passes = [(0, 0, 2), (1, 2, n)]

            def attend(kT_src, v_src, q_list, scale, rope, n):
                nq = len(q_list)
                assert nq <= 2
                if nq == 2 and hb["o"] % 2 == 1:
                    hb["o"] += 1
                accb, obank = [], []
                for qi in range(nq):
                    k_ = (hb["o"] + qi) % 2
                    accb.append([(V3["acc0e"], V3["acc0o"]), (V3["acc1e"], V3["acc1o"])][k_])
                    obank.append((banks[6 + k_], bobj[6 + k_]))
                hb["o"] += nq
                for half in range(2):
                    c0, c1 = half * HCH, min(NCH, (half + 1) * HCH)
                    i = hb["n"] % 2
                    hb["n"] += 1
                    kh, vh = V3["kh%d" % i], V3["vh%d" % i]
                    k0 = c0 * 128
                    nk = (c1 - c0) * 128
                    kb.dma("sp", kh.ap[:, 0:nk], kT_src[:, k0:k0 + nk], reads=[o_kv], writes=[kh.o], semobj=kh.o)
                    nfull = min(c1, NCH - 1) - c0
                    vv = vh.ap.rearrange("p (c d) -> p c d", d=128)
                    if nfull > 0:
                        kb.dma("sp", vv[:, 0:nfull, :], v_src[k0:k0 + nfull * 128, :].rearrange("(c p) d -> p c d", p=128),
                               reads=[o_kv], writes=[vh.o], semobj=vh.o)
                    if c1 == NCH:
                        kb.dma("sp", vv[0:N_META, nfull, :], v_src[c.SEQ:c.SEQ + N_META, :], reads=[o_kv], writes=[vh.o], semobj=vh.o)
                    for qi, (qn_ap, qr_ap, oq, dst_ap, o_dst) in enumerate(q_list):
                        ob_k, ob_o = obank[qi]
                        acc = accb[qi]
                        chunks = list(range(c0, c1))
                        pend = []

                        def qk(cc):
                            nkc = 128 if cc < NCH - 1 else N_META
                            sb_, so_ = pbank()
                            lo = (cc - c0) * 128
                            kb.op("pe", lambda e: e.matmul(sb_[0:nkc, 0:n], kh.ap[:, lo:lo + nkc], qn_ap, start=True, stop=not rope),
                                  reads=[kh.o] + oq, writes=[so_], sig=not rope)
                            if rope:
                                kb.op("pe", lambda e: e.matmul(sb_[0:nkc, 0:n], kr_sb[0:64, cc * 128:cc * 128 + nkc], qr_ap, start=False, stop=True),
                                      reads=[V3["kr"].o] + oq, writes=[so_], sig=True)
                            return (cc, nkc, sb_, so_)

                        def pv(item):
                            cc, nkc, sb_, so_ = item
                            pi = rr["b"] % 4
                            rr["b"] += 1
                            pt = V3["pt%d" % pi]
                            kb.op("act", lambda e: e.activation(pt.ap[0:nkc, 0:n], sb_[0:nkc, 0:n], AF.Exp, scale=scale), reads=[so_], writes=[pt.o])
                            kb.op("pe", lambda e: e.matmul(ob_k[:, 0:n], vv[0:nkc, cc - c0, :], pt.ap[0:nkc, 0:n], start=(cc == 0), stop=(cc == NCH - 1)),
                                  reads=[vh.o, pt.o], writes=[ob_o], sig=True)
                            ac = acc[cc % 2]
                            en = "pool" if cc % 2 == 0 else "dve"
                            if cc < 2:
                                kb.op(en, lambda e: e.tensor_copy(ac.ap[:, 0:n], pt.ap[:, 0:n]), reads=[pt.o], writes=[ac.o])
                            else:
                                kb.op(en, lambda e: e.tensor_tensor(ac.ap[0:nkc, 0:n], ac.ap[0:nkc, 0:n], pt.ap[0:nkc, 0:n], ALU.add),
                                      reads=[pt.o, ac.o], writes=[ac.o])

                        for cc in chunks:
                            pend.append(qk(cc))
                            if len(pend) > 2:
                                pv(pend.pop(0))
                        while pend:
                            pv(pend.pop(0))
                        if half == 1:
                            db, dbo = pbank()
                            kb.op("pe", lambda e: e.matmul(db[:, 0:n], ones_f[:], acc[0].ap[:, 0:n], start=True, stop=False),
                                  reads=[acc[0].o, o_const], writes=[dbo], sig=False)
                            kb.op("pe", lambda e: e.matmul(db[:, 0:n], ones_f[:], acc[1].ap[:, 0:n], start=False, stop=True),
                                  reads=[acc[1].o, o_const], writes=[dbo])
                            rcp = V3["rcp"]
                            kb.op("dve", lambda e: e.reciprocal(rcp.ap[:, 0:n], db[:, 0:n]), reads=[dbo], writes=[rcp.o])
                            kb.op("dve", lambda e: e.tensor_tensor(dst_ap, ob_k[:, 0:n], rcp.ap[:, 0:n], ALU.mult),
                                  reads=[ob_o, rcp.o], writes=[o_dst])

            if ti == 0:
                dbg("qan", qan, V3["qan"].o, [128, c.H, T]); dbg("qar", qar[0:64], V3["qar"].o, [64, c.H, T]); dbg("qb", qb, V3["qb"].o, [128, c.HQ, T])
            sA = 1.0 / math.sqrt(192.0)
            sB = 1.0 / math.sqrt(128.0)
            for (sset, q0, q1) in passes:
                kTa, krT, va, kTb, vb = kTa_[sset], krT_[sset], va_[sset], kTb_[sset], vb_[sset]
                kb.dma("sp", kr_sb[0:64, :], krT[:, :], reads=[o_kv], writes=[V3["kr"].o], semobj=V3["kr"].o)
                for h in range(c.H):
                    attend(kTa[h], va[:, h, :], [(qan[:, h, q0:q1], qar[0:64, h, q0:q1], [V3["qan"].o, V3["qar"].o], oa[:, h, q0:q1], V3["oa"].o)], sA, True, q1 - q0)
                for hk in range(c.HKV):
                    for g0 in range(0, c.G, 2):
                        ql = []
                        for g in range(g0, min(c.G, g0 + 2)):
                            h = hk * c.G + g
                            ql.append((qb[:, h, q0:q1], None, [V3["qb"].o], ob[:, h, q0:q1], V3["ob"].o))
                        attend(kTb[hk], vb[:, hk, :], ql, sB, False, q1 - q0)
            if ti == 0:
                dbg("oa", oa, V3["oa"].o, [128, c.H, T]); dbg("ob", ob, V3["ob"].o, [128, c.HQ, T])
            V4 = stage_views([
                ("oa", 65536, c.H * T * 2, BF16), ("ob", 65536 + c.H * T * 2, c.HQ * T * 2, BF16),
                ("xt0", 0, D * 4, F32), ("xt1", D * 4, D * 4, F32), ("xn", 32768, 4 * D * 2, BF16),
                ("uT", QB, DC * T * 2, BF16), ("junk", QB + DC * T * 2, D * 2, BF16),
            ], carry={"oa": V3["oa"].o, "ob": V3["ob"].o})
            o_xt = [V4["xt0"].o, V4["xt1"].o]
            uT2 = V4["uT"].ap.rearrange("p (c t) -> p c t", c=DC)
            emit_xnormT(x_q[t0:t0 + n, :], n, G_PRE, xt, o_xt, xn, V4["xn"].o, uT2, V4["uT"].o, V4["junk"].ap, V4["junk"].o)
            V5 = stage_views([
                ("oa", 65536, c.H * T * 2, BF16), ("ob", 65536 + c.H * T * 2, c.HQ * T * 2, BF16),
                ("uT", QB, DC * T * 2, BF16), ("mg", 0, DC * T * 2, BF16),
            ], carry={"oa": V3["oa"].o, "ob": V3["ob"].o, "uT": V4["uT"].o})
            mg = V5["mg"].ap.rearrange("p (c t) -> p c t", c=DC)
            rq = []
            for j in range(DC):
                rq += reqs_for("ga", [j]) + reqs_for("pa", [j]) + reqs_for("gb", [j]) + reqs_for("pb", [j])
            ws = WStream(rq)
            for j in range(DC):
                bg, bgo = fm_group(ws, "ga", j, uT2, V5["uT"].o, DC, n)
                ba, bao = fm_group(ws, "pa", j, oa, V5["oa"].o, c.H, n)
                s1, os1 = tf()
                kb.op("act", lambda e: e.activation(s1[:, 0:n], bg[:, 0:n], AF.Sigmoid), reads=[bgo], writes=[os1])
                m1, om1 = tf()
                kb.op("dve", lambda e: e.tensor_tensor(m1[:, 0:n], ba[:, 0:n], s1[:, 0:n], ALU.mult), reads=[bao, os1], writes=[om1])
                bg2, bgo2 = fm_group(ws, "gb", j, uT2, V5["uT"].o, DC, n)
                bb, bbo = fm_group(ws, "pb", j, ob, V5["ob"].o, c.HQ, n)
                s2, os2 = tf()
                kb.op("act", lambda e: e.activation(s2[:, 0:n], bg2[:, 0:n], AF.Sigmoid), reads=[bgo2], writes=[os2])
                m2, om2 = tf()
                kb.op("dve", lambda e: e.tensor_tensor(m2[:, 0:n], bb[:, 0:n], s2[:, 0:n], ALU.mult), reads=[bbo, os2], writes=[om2])
                kb.op("pool", lambda e: e.tensor_tensor(mg[:, j, 0:n], m1[:, 0:n], m2[:, 0:n], ALU.add), reads=[om1, om2], writes=[V5["mg"].o])
            if ti == 0:
                dbg("mg", mg, V5["mg"].o, [128, DC, T])
            V6 = stage_views([
                ("mg", 0, DC * T * 2, BF16), ("dT", 32768, DC * T * 2, BF16),
                ("drow", 65536, D * 4, F32), ("xrow", 65536 + D * 4, D * 4, F32), ("hn", 65536 + 2 * D * 4, D * 2, BF16),
                ("junk", 65536 + 2 * D * 4 + D * 2, D * 2, BF16), ("u2T", 65536 + 2 * D * 4 + 2 * D * 2, DC * T * 2, BF16),
                ("edge", 65536 + 2 * D * 4 + 2 * D * 2 + DC * T * 2, DC * 2 * 2, BF16),
            ], carry={"mg": V5["mg"].o})
            dT = V6["dT"].ap.rearrange("p (c t) -> p c t", c=DC)
            ws = WStream(reqs_for("wo", range(DC)))
            for j in range(DC):
                bk, bo = fm_group(ws, "wo", j, mg, V6["mg"].o, DC, n)
                kb.op("act" if j % 2 else "dve", (lambda e: e.activation(dT[:, j, 0:n], bk[:, 0:n], AF.Copy)) if j % 2 else
                      (lambda e: e.tensor_copy(dT[:, j, 0:n], bk[:, 0:n])), reads=[bo], writes=[V6["dT"].o])
            if ti == 0:
                dbg("dT", dT, V6["dT"].o, [128, DC, T])
            V7 = stage_views([
                ("grow", 0, D * 4, F32), ("dT", 32768, DC * T * 2, BF16),
                ("drow", 65536, D * 4, F32), ("xrow", 65536 + D * 4, D * 4, F32), ("hn", 65536 + 2 * D * 4, D * 2, BF16),
                ("junk", 65536 + 2 * D * 4 + D * 2, D * 2, BF16), ("u2T", 65536 + 2 * D * 4 + 2 * D * 2, DC * T * 2, BF16),
                ("edge", 65536 + 2 * D * 4 + 2 * D * 2 + DC * T * 2, DC * 2 * 2, BF16),
            ], carry={"dT": V6["dT"].o})
            emit_epilogue(V7, dT, n, t0, ti, bl, first=True)

        def emit_epilogue(V6, dT, n, t0, ti, bl, first):
            drow, xrow = V6["drow"], V6["xrow"]
            grow_row = 0 if first else 1
            grow, o_grow = V6["grow"].ap, V6["grow"].o
            kb.dma("sp", grow[:, :], grows[grow_row:grow_row + 1, :].broadcast_to([128, D]), writes=[o_grow], semobj=o_grow)
            src = x_q if first else hscr
            if not first:
                kb_reads = [o_h]
            else:
                kb_reads = []
            u2T = V6["u2T"].ap.rearrange("p (c t) -> p c t", c=DC) if first else None
            for bi, (s0, nt) in enumerate(bl):
                kb.dma("sp", xrow.ap[0:nt, :], src[t0 + s0:t0 + s0 + nt, :], reads=kb_reads, writes=[xrow.o], semobj=xrow.o)
                for c4 in range(0, DC, 4):
                    bk, bo = pbank()
                    bkb = bk.bitcast(BF16)
                    for cc in range(c4, min(DC, c4 + 4)):
                        kb.op("pe", lambda e: e.transpose(bkb[0:nt, (cc - c4) * 128:(cc - c4 + 1) * 128], dT[:, cc, s0:s0 + nt], ident_b[:, :]),
                              reads=[V6["dT"].o, o_const], writes=[bo], sig=(cc == min(DC, c4 + 4) - 1))
                    w = (min(DC, c4 + 4) - c4) * 128
                    kb.op("act" if (c4 // 4) % 2 else "dve",
                          (lambda e: e.activation(drow.ap[0:nt, c4 * 128:c4 * 128 + w], bkb[0:nt, 0:w], AF.Copy)) if (c4 // 4) % 2 else
                          (lambda e: e.tensor_copy(drow.ap[0:nt, c4 * 128:c4 * 128 + w], bkb[0:nt, 0:w])), reads=[bo], writes=[drow.o])
                if not first and EPSTOP <= 1:
                    continue
                kb.op("act", lambda e: e.activation(V6["junk"].ap[0:nt, :], drow.ap[0:nt, :], AF.Square, accum_out=st1[0:nt, 0:1]),
                      reads=[drow.o], writes=[V6["junk"].o, o_st1])
                kb.op("act", lambda e: e.activation(st1[0:nt, 1:2], st1[0:nt, 0:1], AF.Sqrt, bias=EPS, scale=1.0 / D), reads=[o_st1], writes=[o_st1])
                kb.op("dve", lambda e: e.reciprocal(st1[0:nt, 2:3], st1[0:nt, 1:2]), reads=[o_st1], writes=[o_st1])
                if not first and EPSTOP <= 2:
                    continue
                kb.op("dve", lambda e: e.scalar_tensor_tensor(drow.ap[0:nt, :], drow.ap[0:nt, :], st1[0:nt, 2:3], grow[0:nt, :], ALU.mult, ALU.mult),
                      reads=[drow.o, o_st1, o_grow], writes=[drow.o])
                kb.op("pool", lambda e: e.tensor_tensor(drow.ap[0:nt, :], drow.ap[0:nt, :], xrow.ap[0:nt, :], ALU.add),
                      reads=[drow.o, xrow.o], writes=[drow.o])
                if not first:
                    if EPSTOP > 3:
                        kb.dma("sp", y[t0 + s0:t0 + s0 + nt, :], drow.ap[0:nt, :], reads=[drow.o], writes=[o_y], semobj=drow.o)
                    continue
                if ti >= 0:
                    kb.dma("pool", hscr[t0 + s0:t0 + s0 + nt, :], drow.ap[0:nt, :], reads=[drow.o], writes=[o_h], semobj=drow.o)
                hn = V6["hn"]
                kb.op("act", lambda e: e.activation(V6["junk"].ap[0:nt, :], drow.ap[0:nt, :], AF.Square, accum_out=st1[0:nt, 3:4]),
                      reads=[drow.o], writes=[V6["junk"].o, o_st1])
                kb.op("act", lambda e: e.activation(st1[0:nt, 4:5], st1[0:nt, 3:4], AF.Sqrt, bias=EPS, scale=1.0 / D), reads=[o_st1], writes=[o_st1])
                kb.op("dve", lambda e: e.reciprocal(st1[0:nt, 5:6], st1[0:nt, 4:5]), reads=[o_st1], writes=[o_st1])
                kb.op("dve", lambda e: e.tensor_scalar(hn.ap[0:nt, :], drow.ap[0:nt, :], st1[0:nt, 5:6], None, ALU.mult),
                      reads=[drow.o, o_st1], writes=[hn.o])
                for c4 in range(0, DC, 4):
                    bk, bo = pbank()
                    bkb = bk.bitcast(BF16)
                    ce = min(DC, c4 + 4)
                    for cc in range(c4, ce):
                        kb.op("pe", lambda e: e.transpose(bkb[:, (cc - c4) * 128:(cc - c4) * 128 + nt], hn.ap[0:nt, cc * 128:(cc + 1) * 128], ident_b[0:nt, 0:nt]),
                              reads=[hn.o, o_const], writes=[bo], sig=(cc == ce - 1))
                    for cc in range(c4, ce):
                        kb.op("dve" if cc % 2 else "pool" if False else "dve",
                              lambda e: e.tensor_scalar(u2T[:, cc, s0:s0 + nt], bkb[:, (cc - c4) * 128:(cc - c4) * 128 + nt], gc[:, G_FFN + cc:G_FFN + cc + 1], None, ALU.mult),
                              reads=[bo, o_const], writes=[V6["u2T"].o])
            if first and ti >= 0:
                edge = V6["edge"].ap.rearrange("p (c t) -> p c t", t=2)
                kb.op("dve", lambda e: e.tensor_copy(edge[:, :, 0:1], u2T[:, :, 0:1]), reads=[V6["u2T"].o], writes=[V6["edge"].o])
                kb.op("dve", lambda e: e.tensor_copy(edge[:, :, 1:2], u2T[:, :, n - 1:n]), reads=[V6["u2T"].o], writes=[V6["edge"].o])
                kb.dma("pool", u2edge[ti], edge, reads=[V6["edge"].o], writes=[o_edge], semobj=V6["edge"].o)
                kb.dma("pool", u2scr[ti], u2T, reads=[V6["u2T"].o], writes=[o_u2], semobj=V6["u2T"].o)
            if first and ti < 0:
                for cc in range(DC):
                    kb.op("dve", lambda e: e.tensor_tensor(u2T[:, cc, 0:n], u2T[:, cc, 0:n], hmask[:, 0:n], ALU.mult),
                          reads=[V6["u2T"].o, o_const], writes=[V6["u2T"].o])
                kb.dma("pool", u2halo[:, :, 0:n], u2T[:, :, 0:n], reads=[V6["u2T"].o], writes=[o_edge], semobj=V6["u2T"].o)

        o_y = Obj("y")
        if PHASES >= 3:
            attn_tile(c.NQ, c.NH, -1)
            for i in range(c.NT):
                attn_tile(i * T, T, i)

        def ffn_tile(ti):
            t0, n = ti * T, T
            bl = blocks(n)
            FB = c.FC * T * 2
            V = stage_views([
                ("fT", 0, FB, BF16), ("u2", FB, DC * (T + 2) * 2, BF16), ("hal", FB + DC * (T + 2) * 2, DC * 2 * 2 * 2, BF16),
            ])
            fT = V["fT"].ap.rearrange("p (c t) -> p c t", c=c.FC)
            u2 = V["u2"].ap.rearrange("p (c t) -> p c t", c=DC)
            hal = V["hal"].ap.rearrange("p (s c t) -> p s c t", s=2, t=2)
            kb.dma("sp", u2[:, :, 0:T], u2scr[ti], reads=[o_u2], writes=[V["u2"].o], semobj=V["u2"].o)
            un, kk = ti // c.TPU, ti % c.TPU
            if kk == 0:
                kb.dma("sp", hal[:, 0], u2halo[:, :, 2 * un:2 * un + 2], reads=[o_edge], writes=[V["hal"].o], semobj=V["hal"].o)
                lcol = 0
            else:
                kb.dma("sp", hal[:, 0], u2edge[ti - 1], reads=[o_edge], writes=[V["hal"].o], semobj=V["hal"].o)
                lcol = 1
            if kk == c.TPU - 1:
                kb.dma("sp", hal[:, 1], u2halo[:, :, 2 * un:2 * un + 2], reads=[o_edge], writes=[V["hal"].o], semobj=V["hal"].o)
                rcol = 1
            else:
                kb.dma("sp", hal[:, 1], u2edge[ti + 1], reads=[o_edge], writes=[V["hal"].o], semobj=V["hal"].o)
                rcol = 0
            kb.op("dve", lambda e: e.tensor_copy(u2[:, :, T:T + 1], hal[:, 0, :, lcol:lcol + 1]), reads=[V["hal"].o], writes=[V["u2"].o])
            kb.op("dve", lambda e: e.tensor_copy(u2[:, :, T + 1:T + 2], hal[:, 1, :, rcol:rcol + 1]), reads=[V["hal"].o], writes=[V["u2"].o])
            if FFNSTOP <= 1:
                return
            rq = []
            for j in range(c.FC):
                rq += reqs_for("upa", [j]) + reqs_for("upg", [j])
            ws = WStream(rq)
            for j in range(c.FC if FFNSTOP > 1.5 else 2):
                ba, bao = pbank()
                bh, bho = pbank()
                wt, wo_ = ws.get()
                for kc in range(DC):
                    kb.op("pe", lambda e: e.matmul(ba[:, 0:n], wt[:, kc * 128:(kc + 1) * 128], u2[:, kc, 0:n], start=(kc == 0), stop=(kc == DC - 1)),
                          reads=[wo_, V["u2"].o], writes=[bao], sig=False)
                    kb.op("pe", lambda e: e.matmul(bh[:, 0:2], wt[:, kc * 128:(kc + 1) * 128], u2[:, kc, T:T + 2], start=(kc == 0), stop=(kc == DC - 1)),
                          reads=[wo_, V["u2"].o], writes=[bao, bho], sig=(kc == DC - 1))
                bg, bgo = fm_group(ws, "upg", j, u2, V["u2"].o, DC, n)
                a_sb, oa_sb = tf()
                kb.op("act", lambda e: e.activation(a_sb[:, 0:n], ba[:, 0:n], AF.Copy), reads=[bao], writes=[oa_sb])
                cv, ocv = tf()
                w0 = gc[:, G_CW + 3 * j:G_CW + 3 * j + 1]; w1 = gc[:, G_CW + 3 * j + 1:G_CW + 3 * j + 2]; w2 = gc[:, G_CW + 3 * j + 2:G_CW + 3 * j + 3]
                bcol = gc[:, G_CB + j:G_CB + j + 1]
                kb.op("dve", lambda e: e.tensor_scalar(cv[:, 0:n], ba[:, 0:n], w1, bcol, ALU.mult, ALU.add), reads=[bao, o_const], writes=[ocv])
                kb.op("dve", lambda e: e.scalar_tensor_tensor(cv[:, 1:n], a_sb[:, 0:n - 1], w0, cv[:, 1:n], ALU.mult, ALU.add), reads=[oa_sb, ocv, o_const], writes=[ocv])
                kb.op("dve", lambda e: e.scalar_tensor_tensor(cv[:, 0:n - 1], a_sb[:, 1:n], w2, cv[:, 0:n - 1], ALU.mult, ALU.add), reads=[oa_sb, ocv, o_const], writes=[ocv])
                kb.op("dve", lambda e: e.scalar_tensor_tensor(cv[:, 0:1], bh[:, 0:1], w0, cv[:, 0:1], ALU.mult, ALU.add), reads=[bho, ocv, o_const], writes=[ocv])
                kb.op("dve", lambda e: e.scalar_tensor_tensor(cv[:, n - 1:n], bh[:, 1:2], w2, cv[:, n - 1:n], ALU.mult, ALU.add), reads=[bho, ocv, o_const], writes=[ocv])
                ge, oge = tf()
                kb.op("act", lambda e: e.activation(ge[:, 0:n], cv[:, 0:n], AF.Gelu_apprx_tanh), reads=[ocv], writes=[oge])
                kb.op("dve", lambda e: e.tensor_tensor(fT[:, j, 0:n], bg[:, 0:n], ge[:, 0:n], ALU.mult), reads=[bgo, oge], writes=[V["fT"].o])
            if FFNSTOP <= 2:
                return
            V2 = stage_views([("fT", 0, FB, BF16), ("dT", FB, DC * T * 2, BF16)], carry={"fT": V["fT"].o})
            dT = V2["dT"].ap.rearrange("p (c t) -> p c t", c=DC)
            ws = WStream(reqs_for("dn", range(DC)))
            for j in range(DC):
                bk, bo = fm_group(ws, "dn", j, fT, V2["fT"].o, c.FC, n)
                kb.op("act" if j % 2 else "dve", (lambda e: e.activation(dT[:, j, 0:n], bk[:, 0:n], AF.Copy)) if j % 2 else
                      (lambda e: e.tensor_copy(dT[:, j, 0:n], bk[:, 0:n])), reads=[bo], writes=[V2["dT"].o])
            if FFNSTOP <= 3:
                return
            E0 = 0 if FB >= 3 * D * 4 + D * 2 else FB + DC * T * 2
            V6 = stage_views([("dT", FB, DC * T * 2, BF16), ("drow", E0, D * 4, F32), ("xrow", E0 + D * 4, D * 4, F32), ("junk", E0 + 2 * D * 4, D * 2, BF16),
                              ("grow", E0 + 2 * D * 4 + D * 2, D * 4, F32)],
                             carry={"dT": V2["dT"].o})
            emit_epilogue(V6, dT, n, t0, ti, bl, first=False)

        if PHASES >= 4:
            for i in range(c.NT):
                ffn_tile(i)
        for o_ in [o_kv, o_h, o_u2, o_edge] + [p_[3] for p_ in pieces]:
            kb._wait("pool", o_.w)
        kb._wait("pool", o_y.w)
        kb._wait("sp", o_y.w)
        global LAST_STATS
        LAST_STATS = dict(cnt=dict(kb.cnt), max_dma=max(kb.issued.values()), nsem=len(kb.sems))
    return nc


def make_inputs(c, x_seq, meta_tokens, P):
    perm = _axial_perm()
    NG = 2 * c.DC + c.QC + c.KC + 2 + 4 * c.FC
    g = np.zeros((128, NG), np.float32)
    col = lambda v: np.asarray(v, np.float32).reshape(-1, 128).T
    g[:, 0:c.DC] = col(P["g_pre_mix"]); g[:, c.DC:2 * c.DC] = col(P["g_pre_ffn"])
    o = 2 * c.DC
    g[:, o:o + c.QC] = col(P["g_cq"]); o += c.QC
    g[:, o:o + c.KC] = col(P["g_ckv"]); o += c.KC
    g[:, o] = np.asarray(P["g_qn"], np.float32).reshape(128)[perm]; o += 1
    g[:, o] = np.asarray(P["g_kn"], np.float32).reshape(128)[perm]; o += 1
    wc = np.asarray(P["w_conv"], np.float32).reshape(3, c.FC, 128)
    g[:, o:o + 3 * c.FC] = wc.transpose(2, 1, 0).reshape(128, 3 * c.FC); o += 3 * c.FC
    g[:, o:o + c.FC] = col(P["b_conv"])
    grows = np.stack([np.asarray(P["g_post_mix"], np.float32).reshape(-1), np.asarray(P["g_post_ffn"], np.float32).reshape(-1)])
    return g, grows


_PROG = {}


def core_plan(c):
    plan = []
    for core in range(8):
        g, c4 = divmod(core, 4)
        s0, s1, s2 = 3 * g, 3 * g + 1, 3 * g + 2
        if c4 == 0:
            A, B, units = s0, s0, [(s0, 0), (s0, 1), (s0, 2)]
        elif c4 == 1:
            A, B, units = s0, s1, [(s0, 3), (s1, 0), (s1, 1)]
        elif c4 == 2:
            A, B, units = s2, s1, [(s2, 0), (s1, 2), (s1, 3)]
        else:
            A, B, units = s2, s2, [(s2, 1), (s2, 2), (s2, 3)]
        plan.append((A, B, units))
    return plan


def run(c, x_prompt, x_sample, meta_tokens, P, wts):
    key = (c.D, c.SEQ, c.T, c.DFF)
    if key not in _PROG:
        _PROG[key] = build_program(c)
    nc = _PROG[key]
    seqs = [np.asarray(x_prompt[i], np.float32) for i in range(x_prompt.shape[0])] + [np.asarray(x_sample[i], np.float32) for i in range(x_sample.shape[0])]
    assert len(seqs) == 6
    wall = build_wall(c, *wts)
    ta, tb = rope_tables(c, *kv_positions(c), c.LP)
    g, grows = make_inputs(c, None, meta_tokens, P)
    ident = np.eye(128, dtype=np.float32)
    meta = np.asarray(meta_tokens, np.float32)
    plan = core_plan(c)
    U = c.UNIT
    in_maps = []
    for core in range(8):
        A, B, units = plan[core]
        x_kv = np.stack([np.concatenate([seqs[A], meta], axis=0), np.concatenate([seqs[B], meta], axis=0)])
        rows, pos, rid, cid = [], [], [], []
        hrows, hpos, hrid, hcid, hm = [], [], [], [], []
        for (sq, ch) in units:
            t = np.arange(ch * U, (ch + 1) * U)
            rows.append(seqs[sq][t]); pos.append(t + N_META); rid.append(t // GRID_W); cid.append(t % GRID_W)
            if ch == 0:
                hrows.append(meta[N_META - 1]); hpos.append(N_META - 1); hrid.append(0); hcid.append(0); hm.append(1.0)
            else:
                tl = ch * U - 1
                hrows.append(seqs[sq][tl]); hpos.append(tl + N_META); hrid.append(tl // GRID_W); hcid.append(tl % GRID_W); hm.append(1.0)
            if ch == c.SEQ // U - 1:
                hrows.append(seqs[sq][0]); hpos.append(N_META); hrid.append(0); hcid.append(0); hm.append(0.0)
            else:
                tr = (ch + 1) * U
                hrows.append(seqs[sq][tr]); hpos.append(tr + N_META); hrid.append(tr // GRID_W); hcid.append(tr % GRID_W); hm.append(1.0)
        x_q = np.ascontiguousarray(np.concatenate(rows + [np.stack(hrows)], axis=0))
        qpos = np.concatenate(pos + [np.array(hpos)]).astype(np.float32)
        qrid = np.concatenate(rid + [np.array(hrid)]).astype(np.float32)
        qcid = np.concatenate(cid + [np.array(hcid)]).astype(np.float32)
        tqa, tqb = rope_tables(c, qpos, qrid, qcid, c.NQP)
        hmask = np.zeros((1, 8), np.float32); hmask[0, :len(hm)] = hm
        in_maps.append({"x_kv": x_kv, "x_q": x_q, "tqA": tqa, "tqB": tqb, "hmask": hmask, "wall": wall, "tabA": ta, "tabB": tb,
                        "gcols": g, "grows": grows, "ident": ident})
    res = run_bass_kernel_spmd(nc, in_maps, core_ids=list(range(8)))
    outs = [np.empty((c.SEQ, c.D), np.float32) for _ in range(6)]
    for core in range(8):
        yc = res.results[core]["y"]
        for ui, (sq, ch) in enumerate(plan[core][2]):
            outs[sq][ch * U:(ch + 1) * U] = yc[ui * U:(ui + 1) * U]
    nb = x_prompt.shape[0]
    return np.stack(outs[:nb]).astype(np.float32), np.stack(outs[nb:]).astype(np.float32)


def kernel(x_prompt, x_sample, meta_tokens, g_pre_mix, w_in, g_cq, w_uq, g_ckv, w_ukv, g_qn, g_kn,
           w_pa, w_pb, w_o, g_post_mix, g_pre_ffn, w_up, w_conv, b_conv, w_down, g_post_ffn):
    c = FULL
    A = lambda a: np.asarray(a, np.float32)
    P = dict(g_pre_mix=A(g_pre_mix)[0], g_cq=A(g_cq)[0], g_ckv=A(g_ckv)[0], g_qn=A(g_qn)[0], g_kn=A(g_kn)[0],
             g_post_mix=A(g_post_mix)[0], g_pre_ffn=A(g_pre_ffn)[0], w_conv=A(w_conv)[0], b_conv=A(b_conv)[0], g_post_ffn=A(g_post_ffn)[0])
    wts = (A(w_in)[0], A(w_uq)[0], A(w_ukv)[0], A(w_pa)[0], A(w_pb)[0], A(w_o)[0], A(w_up)[0], A(w_down)[0])
    return run(c, A(x_prompt), A(x_sample), A(meta_tokens), P, wts)
```
